# Optimizing a Trainium2 kernel written in Bass

```python
import math
import jax, jax.numpy as jnp
from jax import lax
import numpy as np

D_MODEL = 1024
BATCH = 4
SEQ = 8192
DEPTH = 2

BRANCH_WIDTH = D_MODEL
N_BRANCHES = 3
CHUNK = 128
GMLP_GROUPS = 8
GMLP_GROUP_DIM = BRANCH_WIDTH // GMLP_GROUPS
CONV_WIDTH = 31
XATTN_HEADS = 4
XATTN_HEAD_DIM = BRANCH_WIDTH // XATTN_HEADS
MEM_LEN = 256
OFF_A_U = 0
OFF_A_V = OFF_A_U + BRANCH_WIDTH
OFF_A_G = OFF_A_V + BRANCH_WIDTH
OFF_B_A = OFF_A_G + BRANCH_WIDTH
OFF_B_B = OFF_B_A + BRANCH_WIDTH
OFF_B_G = OFF_B_B + BRANCH_WIDTH
OFF_C_Q = OFF_B_G + BRANCH_WIDTH
OFF_C_G = OFF_C_Q + BRANCH_WIDTH
OFF_MERGE = OFF_C_G + BRANCH_WIDTH
N_IN = OFF_MERGE + N_BRANCHES * D_MODEL
RMS_EPS = 1e-6
LN_EPS = 1e-5

kernel_name = "hybrid_gmlp_conformer_xattn_gated_merge"


def rms_norm(x, g):
    xf = x.astype(jnp.float32)
    y = xf * lax.rsqrt(jnp.mean(xf * xf, axis=-1, keepdims=True) + RMS_EPS)
    return (y * g.astype(jnp.float32)).astype(x.dtype)


def layer_norm(x, g, b):
    xf = x.astype(jnp.float32)
    mu = jnp.mean(xf, axis=-1, keepdims=True)
    xc = xf - mu
    var = jnp.mean(xc * xc, axis=-1, keepdims=True)
    y = xc * lax.rsqrt(var + LN_EPS)
    return (y * g.astype(jnp.float32) + b.astype(jnp.float32)).astype(x.dtype)


def gmlp_spatial_gate(u, v, w_s, b_s):
    bsz, seq, width = v.shape
    n_chunks = seq // CHUNK
    mask = jnp.tril(jnp.ones((CHUNK, CHUNK), dtype=bool))
    ws = jnp.where(mask[None], w_s, 0.0).astype(v.dtype)
    vr = v.reshape(bsz, n_chunks, CHUNK, GMLP_GROUPS, GMLP_GROUP_DIM)
    sv = jnp.einsum('gts,bcsgd->bctgd', ws, vr) + b_s.T.astype(v.dtype)[None, None, :, :, None]
    return u * sv.reshape(bsz, seq, width)


def causal_depthwise_conv(x, w, b):
    k = w.astype(x.dtype)[:, None, :]
    y = lax.conv_general_dilated(
        x, k, window_strides=(1,), padding=[(CONV_WIDTH - 1, 0)],
        dimension_numbers=('NWC', 'WIO', 'NWC'), feature_group_count=x.shape[-1])
    return y + b.astype(x.dtype)


def memory_cross_attention(q, mem_n, w_kv):
    bsz, seq, _ = q.shape
    kv = jnp.einsum('bmd,de->bme', mem_n, w_kv).reshape(bsz, MEM_LEN, 2, XATTN_HEADS, XATTN_HEAD_DIM)
    k, v = kv[:, :, 0], kv[:, :, 1]
    qh = q.reshape(bsz, seq, XATTN_HEADS, XATTN_HEAD_DIM)
    scores = jnp.einsum('bshd,bmhd->bhsm', qh.astype(jnp.float32), k.astype(jnp.float32))
    probs = jax.nn.softmax(scores * (1.0 / math.sqrt(XATTN_HEAD_DIM)), axis=-1).astype(v.dtype)
    out = jnp.einsum('bhsm,bmhd->bshd', probs, v)
    return out.reshape(bsz, seq, BRANCH_WIDTH)


def setup_inputs(seed: int = 0) -> dict:
    key = jax.random.key(seed)
    ks = jax.random.split(key, 20)
    f32 = jnp.float32
    nrm = lambda k, shape, scale: jax.random.normal(k, shape, f32) * scale
    return {
        "x": nrm(ks[0], (BATCH, SEQ, D_MODEL), 1.0),
        "mem": nrm(ks[1], (BATCH, MEM_LEN, D_MODEL), 1.0),
        "norm_g": 1.0 + nrm(ks[2], (DEPTH, D_MODEL), 0.02),
        "mem_norm_g": 1.0 + nrm(ks[3], (DEPTH, D_MODEL), 0.02),
        "w_in": nrm(ks[4], (DEPTH, D_MODEL, N_IN), D_MODEL ** -0.5),
        "gmlp_ln_g": 1.0 + nrm(ks[5], (DEPTH, BRANCH_WIDTH), 0.02),
        "gmlp_ln_b": nrm(ks[6], (DEPTH, BRANCH_WIDTH), 0.02),
        "w_s": nrm(ks[7], (DEPTH, GMLP_GROUPS, CHUNK, CHUNK), CHUNK ** -0.5),
        "b_s": 1.0 + nrm(ks[8], (DEPTH, GMLP_GROUPS, CHUNK), 0.02),
        "conv_w": nrm(ks[9], (DEPTH, CONV_WIDTH, BRANCH_WIDTH), CONV_WIDTH ** -0.5),
        "conv_b": nrm(ks[10], (DEPTH, BRANCH_WIDTH), 0.02),
        "conv_ln_g": 1.0 + nrm(ks[11], (DEPTH, BRANCH_WIDTH), 0.02),
        "conv_ln_b": nrm(ks[12], (DEPTH, BRANCH_WIDTH), 0.02),
        "w_kv": nrm(ks[13], (DEPTH, D_MODEL, 2 * BRANCH_WIDTH), D_MODEL ** -0.5),
        "w_branch": nrm(ks[14], (DEPTH, N_BRANCHES, BRANCH_WIDTH, D_MODEL), BRANCH_WIDTH ** -0.5),
        "w_out": nrm(ks[15], (DEPTH, D_MODEL, D_MODEL), D_MODEL ** -0.5),
        "final_norm_g": 1.0 + nrm(ks[16], (D_MODEL,), 0.02),
    }


def reference(x, mem, norm_g, mem_norm_g, w_in, gmlp_ln_g, gmlp_ln_b, w_s, b_s,
              conv_w, conv_b, conv_ln_g, conv_ln_b, w_kv, w_branch, w_out, final_norm_g):
    W = BRANCH_WIDTH
    for l in range(DEPTH):
        h = rms_norm(x, norm_g[l])
        z = jnp.einsum('bsd,de->bse', h, w_in[l])

        u = jax.nn.gelu(z[..., OFF_A_U:OFF_A_U + W])
        v = layer_norm(jax.nn.gelu(z[..., OFF_A_V:OFF_A_V + W]), gmlp_ln_g[l], gmlp_ln_b[l])
        br_a = gmlp_spatial_gate(u, v, w_s[l], b_s[l]) * jax.nn.silu(z[..., OFF_A_G:OFF_A_G + W])

        glu = z[..., OFF_B_A:OFF_B_A + W] * jax.nn.sigmoid(z[..., OFF_B_B:OFF_B_B + W])
        c = causal_depthwise_conv(glu, conv_w[l], conv_b[l])
        c = jax.nn.silu(layer_norm(c, conv_ln_g[l], conv_ln_b[l]))
        br_b = c * jax.nn.silu(z[..., OFF_B_G:OFF_B_G + W])

        mem_n = rms_norm(mem, mem_norm_g[l])
        att = memory_cross_attention(z[..., OFF_C_Q:OFF_C_Q + W], mem_n, w_kv[l])
        br_c = att * jax.nn.silu(z[..., OFF_C_G:OFF_C_G + W])

        branches = jnp.stack([br_a, br_b, br_c], axis=2)
        proj = jnp.einsum('bsnw,nwd->bsnd', branches, w_branch[l])
        gates = jax.nn.sigmoid(z[..., OFF_MERGE:OFF_MERGE + N_BRANCHES * D_MODEL]).reshape(
            x.shape[0], x.shape[1], N_BRANCHES, D_MODEL)
        merged = jnp.einsum('bsnd,bsnd->bsd', gates, proj)
        x = x + jnp.einsum('bsd,de->bse', merged, w_out[l])
    return rms_norm(x, final_norm_g)
```

```python
import contextlib
import json
import os
import numpy as np
import concourse.bass as bass
import concourse.mybir as mybir
from concourse.bass_utils import run_bass_kernel_spmd

F32 = mybir.dt.float32
BF16 = mybir.dt.bfloat16
AF = mybir.ActivationFunctionType
ALU = mybir.AluOpType


class Tok:
    __slots__ = ("name", "lw", "rd")

    def __init__(self, name):
        self.name = name
        self.lw = None
        self.rd = []


class Buf:
    def __init__(self, ap, name):
        self.ap = ap
        self.name = name
        self.t = Tok(name)
        self._subs = {}

    def s(self, i):
        if i not in self._subs:
            self._subs[i] = Tok(f"{self.name}[{i}]")
        return self._subs[i]


class Op:
    __slots__ = ("idx", "eng", "emit", "dma_key", "deps", "sig", "final", "needs_sig", "tag", "cost", "nbytes",
                 "start", "finish", "pos")

    def __init__(self, idx, eng, emit, dma_key, final):
        self.idx = idx
        self.eng = eng
        self.emit = emit
        self.dma_key = dma_key
        self.deps = {}
        self.sig = None
        self.final = final
        self.needs_sig = False
        self.tag = ""
        self.cost = 0.3
        self.nbytes = 0
        self.start = 0.0
        self.finish = 0.0
        self.pos = 0


class Prog:
    ENGS = ("sp", "act", "dve", "pool", "pe")
    HOP = 0.12
    DMA_BW = 330e3
    DMA_LAT = 2.0
    WINDOW = 24

    def __init__(self, nc):
        self.nc = nc
        self.ops = []
        self.toks = {}
        self.stack = contextlib.ExitStack()
        self.phase = ""
        self.names = None
        self.reorder = True

    def sb(self, name, shape, dt):
        h = self.stack.enter_context(self.nc.sbuf_tensor(name, list(shape), dt))
        return Buf(h, name)

    def ps(self, name, shape, dt):
        h = self.stack.enter_context(self.nc.psum_tensor(name, list(shape), dt))
        return Buf(h, name)

    def tok(self, name):
        if name not in self.toks:
            self.toks[name] = Tok(name)
        return self.toks[name]

    def _record(self, eng, emit, reads, writes, dma_key=None, final=False, cost=0.3, nbytes=0):
        idx = len(self.ops)
        op = Op(idx, eng, emit, dma_key, final)
        op.tag = self.phase
        op.cost = cost
        op.nbytes = nbytes
        for t in reads:
            if t.lw is not None:
                op.deps[t.lw] = "raw"
        for t in writes:
            if t.lw is not None and t.lw not in op.deps:
                op.deps[t.lw] = "waw"
            for r in t.rd:
                if r not in op.deps and r != idx:
                    op.deps[r] = "war"
        for t in reads:
            t.rd.append(idx)
        for t in writes:
            t.rd = []
            t.lw = idx
        self.ops.append(op)
        return op

    def op(self, eng, emit, reads=(), writes=(), cost=0.3):
        return self._record(eng, emit, list(reads), list(writes), cost=cost)

    def dma(self, eng, _unused, emit, key, reads=(), writes=(), final=False, nbytes=1 << 20):
        return self._record(eng, emit, list(reads), list(writes), dma_key=key, final=final, cost=0.1, nbytes=nbytes)

    def _needs_sem(self, a, b, kind):
        if a.dma_key is not None or b.dma_key is not None:
            return True
        if a.eng != b.eng:
            return True
        if a.eng == "pe":
            return False
        return kind == "raw"

    def _schedule(self):
        ops = self.ops
        queues = {e: [o.idx for o in ops if o.eng == e] for e in self.ENGS}
        head = {e: 0 for e in self.ENGS}
        t_free = {e: 0.0 for e in self.ENGS}
        done = [False] * len(ops)
        dma_pipe = 0.0
        order = {e: [] for e in self.ENGS}
        remaining = len(ops)
        W = self.WINDOW
        while remaining:
            best = None
            for e in self.ENGS:
                q = queues[e]
                h = head[e]
                while h < len(q) and done[q[h]]:
                    h += 1
                head[e] = h
                if h >= len(q):
                    continue
                cnt = 0
                i = h
                tf = t_free[e]
                while i < len(q) and cnt < W:
                    oi = q[i]
                    i += 1
                    if done[oi]:
                        continue
                    cnt += 1
                    o = ops[oi]
                    st = tf
                    ok = True
                    for ai in o.deps:
                        if not done[ai]:
                            ok = False
                            break
                        a = ops[ai]
                        f = a.finish + (self.HOP if (a.eng != e or a.dma_key is not None) else 0.0)
                        if f > st:
                            st = f
                    if not ok:
                        continue
                    if best is None or st < best[0] - 1e-9 or (abs(st - best[0]) <= 1e-9 and oi < best[1]):
                        best = (st, oi, e)
                    if st <= tf + 1e-9:
                        break
            st, oi, e = best
            o = ops[oi]
            o.start = st
            if o.dma_key is not None:
                xfer0 = max(st + 0.1, dma_pipe)
                dma_pipe = xfer0 + o.nbytes / self.DMA_BW
                o.finish = dma_pipe + self.DMA_LAT
                t_free[e] = st + o.cost
            else:
                o.finish = st + o.cost
                t_free[e] = o.finish
            done[oi] = True
            order[e].append(oi)
            remaining -= 1
        self.sim_time = max(o.finish for o in ops)
        return order

    def emit(self):
        nc = self.nc
        ops = self.ops
        if self.reorder:
            order = self._schedule()
        else:
            order = {e: [o.idx for o in ops if o.eng == e] for e in self.ENGS}
        for e in self.ENGS:
            for p, oi in enumerate(order[e]):
                ops[oi].pos = p
        for b in ops:
            kept = {}
            latest = {}
            for ai, kind in b.deps.items():
                a = ops[ai]
                if not self._needs_sem(a, b, kind):
                    continue
                if a.dma_key is not None:
                    kept[ai] = kind
                elif a.eng not in latest or a.pos > ops[latest[a.eng]].pos:
                    latest[a.eng] = ai
            for ai in latest.values():
                kept[ai] = b.deps[ai]
            for ai in kept:
                ops[ai].needs_sig = True
            b.deps = kept
        cnt = {}
        keys = []
        for e in self.ENGS:
            for oi in order[e]:
                o = ops[oi]
                if o.dma_key is not None:
                    k = ("dma", o.dma_key)
                    cnt[k] = cnt.get(k, 0) + 16
                    o.sig = (k, cnt[k])
                elif o.needs_sig or o.final:
                    k = ("eng", o.eng)
                    cnt[k] = cnt.get(k, 0) + 1
                    o.sig = (k, cnt[k])
                else:
                    continue
                if k not in keys:
                    keys.append(k)
        sems = {}
        for k in keys:
            sems[k] = self.stack.enter_context(nc.semaphore(f"s_{k[0]}_{k[1]}".replace(" ", "")))
        self.n_sems = len(keys)
        finals = [o for o in ops if o.final]

        def run_stream(ename, e):
            seen = {}
            for oi in order[ename]:
                o = ops[oi]
                need = {}
                for ai in o.deps:
                    k, v = ops[ai].sig
                    if v > need.get(k, 0):
                        need[k] = v
                for k, v in need.items():
                    if seen.get(k, 0) >= v:
                        continue
                    e.wait_ge(sems[k], v)
                    seen[k] = v
                ins = o.emit(e)
                if self.names is not None:
                    self.names[ins.ins.name] = o.tag
                if o.sig is not None:
                    k, v = o.sig
                    ins.then_inc(sems[k], 16 if k[0] == "dma" else 1)
            if ename == "sp":
                for o in finals:
                    k, v = o.sig
                    if seen.get(k, 0) < v:
                        e.wait_ge(sems[k], v)
                        seen[k] = v

        with nc.Block() as block:
            @block.sync
            def _(e):
                run_stream("sp", e)

            @block.scalar
            def _(e):
                run_stream("act", e)

            @block.vector
            def _(e):
                run_stream("dve", e)

            @block.gpsimd
            def _(e):
                run_stream("pool", e)

            @block.tensor
            def _(e):
                run_stream("pe", e)
        self.stack.close()


D = 1024
NIN = 11264
SEQ = 8192
BATCH = 4
MEM = 256
NCORES = 8
TOK_CORE = 4096
HALO = 256
NCH_CORE = (TOK_CORE + HALO) // 128
KW = 31
RMS_EPS = 1e-6
LN_EPS = 1e-5
BLK_U, BLK_V, BLK_AG, BLK_BA, BLK_BB, BLK_BG, BLK_CQ, BLK_CG, BLK_MG = 0, 2, 4, 6, 8, 10, 12, 14, 16
BLK_KV, BLK_BR, BLK_OUT, NBLK = 22, 26, 32, 34
RING = 4
FAST_RCP = False
B3EARLY = os.environ.get('MK_B3EARLY', '1') == '1'
PACE = os.environ.get('MK_PACE', '1') == '1'
SPLIT_EVAC = os.environ.get('MK_SPLIT', '1') == '1'
FSQ = os.environ.get('MK_FSQ', '1') == '1'
FSQ2 = os.environ.get('MK_FSQ2', '1') == '1'
NTMP = 8


class Rot:
    def __init__(self, bufs):
        self.bufs = bufs
        self.i = 0

    def next(self):
        b = self.bufs[self.i]
        self.i = (self.i + 1) % len(self.bufs)
        return b


def build_program():
    nc = bass.Bass("TRN2", target_bir_lowering=False)
    dt = nc.dram_tensor
    x_in = dt("x_in", [NCH_CORE * 128, D], F32, kind="ExternalInput").ap()
    mem_in = dt("mem_in", [MEM, D], F32, kind="ExternalInput").ap()
    w_in = dt("w_in", [2, D, NIN], F32, kind="ExternalInput").ap()
    w_kv = dt("w_kv", [2, D, 2 * D], F32, kind="ExternalInput").ap()
    w_br = dt("w_br", [2, 3, D, D], F32, kind="ExternalInput").ap()
    w_out = dt("w_out", [2, D, D], F32, kind="ExternalInput").ap()
    gb_in = dt("gb_in", [7, 128, D], F32, kind="ExternalInput").ap()
    pc_in = dt("pc_in", [128, 64], F32, kind="ExternalInput").ap()
    cw_in = dt("cw_in", [128, 2 * 8 * KW], F32, kind="ExternalInput").ap()
    wst_in = dt("wst_in", [2, 128, 1024], F32, kind="ExternalInput").ap()
    bs_in = dt("bs_in", [2, 1, 1024], F32, kind="ExternalInput").ap()
    cst_in = dt("cst_in", [128, 384], F32, kind="ExternalInput").ap()
    y_out = dt("y_out", [TOK_CORE, D], F32, kind="ExternalOutput").ap()
    wsc = dt("wsc", [2 * NBLK, 128, 4096], BF16, kind="Internal").ap()
    dsc = dt("dsc", [16, 128, KW * 128], BF16, kind="Internal").ap()

    P = Prog(nc)
    cst = P.sb("cst", [128, 384], F32)
    ident_b = P.sb("ident_b", [128, 128], BF16)
    ones_b = P.sb("ones_b", [128, 128], BF16)
    pc = P.sb("pc", [128, 64], F32)
    pch = P.sb("pch", [128, 32], F32)
    gbn = [P.sb(f"gbn{i}", [128, D], F32) for i in range(3)]
    KT = [P.sb(f"KT{l}", [128, 8, MEM], BF16) for l in range(2)]
    VV = [P.sb(f"VV{l}", [128, 2, D], BF16) for l in range(2)]
    wsTb = [P.sb(f"wsTb{l}", [128, 8, 128], BF16) for l in range(2)]
    biasA = [P.sb(f"biasA{l}", [128, 8, 128], F32) for l in range(2)]
    glu = [P.sb(f"glu{l}", [128, 8, 30 + 512], BF16) for l in range(2)]
    x_tm = P.sb("x_tm", [128, 4, D], F32)
    xs_rot = Rot([P.sb(f"xs{i}", [128, D], BF16) for i in range(3)])
    hT = P.sb("hT", [128, 8, 512], BF16)
    brA = P.sb("brA", [128, 8, 512], BF16)
    brB = P.sb("brB", [128, 8, 512], BF16)
    brC = P.sb("brC", [128, 4096], BF16)
    qm = P.sb("qm", [128, 8, 512], BF16)
    yB = P.sb("yB", [128, 4096], F32)
    ybf_rot = Rot([P.sb(f"ybf{i}", [128, 512], BF16) for i in range(2)])
    ysq_rot = Rot([P.sb(f"ysq{i}", [128, 512], BF16) for i in range(2)])
    pt_rot = Rot([P.sb(f"PT{i}", [128, 2, 512], BF16) for i in range(2)])
    tmp = Rot([P.sb(f"tmp{i}", [128, 512], F32) for i in range(NTMP)])
    macc = [P.sb(f"macc{i}", [128, 512], F32) for i in range(4)]
    s2rot = Rot(macc)
    meanT = P.sb("meanT", [128, 512], F32)
    rstdT = P.sb("rstdT", [128, 512], F32)
    ring = [P.sb(f"ring{i}", [128, 4096], BF16) for i in range(RING)]
    sm = P.sb("sm", [128, 32], F32)
    smP = P.sb("smP", [128, 16], F32)
    smF = P.sb("smF", [128, 16], F32)
    epsR = P.sb("epsR", [128, 2], F32)
    st6 = P.sb("st6", [128, 2, 6], F32)
    mv = P.sb("mv", [128, 4, 2], F32)
    zrot = Rot([P.ps(f"zp{i}", [128, 512], F32) for i in range(5)])
    tp = P.ps("tp", [128, 8, 128], BF16)
    s1p = P.ps("s1p", [128, 512], F32)
    s2p = P.ps("s2p", [128, 512], F32)

    hTt = [hT.s(0), hT.s(1)]
    a_bf = brC.ap[:, :].rearrange("p (c d) -> p c d", d=D)
    brC3 = brC.ap[:, :].rearrange("p (f t) -> p f t", t=512)
    yB3 = yB.ap[:, :].rearrange("p (f t) -> p f t", t=512)
    yB4 = yB.ap[:, :].rearrange("p (c d) -> p c d", d=D)
    ident_f = cst.ap[:, 0:128]
    tril_f = cst.ap[:, 128:256]
    ones_f = cst.ap[:, 256:384]

    def fs(ap):
        n = 1
        for d in ap.shape[1:]:
            n *= d
        return n

    def ACT(out, in_, func, reads, writes, scale=1.0, bias=None, accum=None):
        kw = {}
        if bias is not None:
            kw["bias"] = bias
        if accum is not None:
            kw["accum_out"] = accum
        return P.op("act", lambda e: e.activation(out=out, in_=in_, func=func, scale=scale, **kw), reads, writes,
                    cost=0.25 + fs(out) / 1200.0)

    def vcost(eng, out, fast=False):
        n = fs(out)
        if eng == "pool":
            return 0.2 + n * 0.0021
        return 0.16 + n / (1920.0 if fast else 960.0)

    def TT(out, in0, in1, op, reads, writes, eng="dve"):
        return P.op(eng, lambda e: e.tensor_tensor(out=out, in0=in0, in1=in1, op=op), reads, writes, cost=vcost(eng, out))

    def STT(out, in0, scalar, in1, op0, op1, reads, writes, eng="dve"):
        return P.op(eng, lambda e: e.scalar_tensor_tensor(out=out, in0=in0, scalar=scalar, in1=in1, op0=op0, op1=op1), reads, writes,
                    cost=vcost(eng, out))

    def TS(out, in0, s1, s2, op0, op1, reads, writes, eng="dve"):
        return P.op(eng, lambda e: e.tensor_scalar(out=out, in0=in0, scalar1=s1, scalar2=s2, op0=op0, op1=op1), reads, writes,
                    cost=vcost(eng, out, fast=True))

    def CP(out, in_, reads, writes, eng="dve"):
        return P.op(eng, lambda e: e.tensor_copy(out=out, in_=in_), reads, writes, cost=vcost(eng, out))

    def RCP(out, in_, reads, writes):
        if fs(out) >= 64 and FAST_RCP:
            return P.op("dve", lambda e: e.reciprocal_approx_fast(out=out, in_=in_), reads, writes, cost=0.16 + fs(out) * 0.0013)
        return P.op("dve", lambda e: e.reciprocal(out=out, in_=in_), reads, writes, cost=0.1 + fs(out) * 0.0065)

    def MM(out, lhsT, rhs, start, stop, reads, writes):
        return P.op("pe", lambda e: e.matmul(out, lhsT=lhsT, rhs=rhs, start=start, stop=stop), reads, writes,
                    cost=0.008 + max(fs(out), 64) / 2400.0)

    wtok = [P.tok(f"wsc{j}") for j in range(2 * NBLK)]
    dtok = [P.tok(f"dsc{j}") for j in range(16)]

    class WS:
        res = {}
        slot_key = [None] * RING
        nxt = 0

    def wget(key):
        if key in WS.res:
            return ring[WS.res[key]]
        s = WS.nxt
        WS.nxt = (s + 1) % RING
        if WS.slot_key[s] is not None:
            del WS.res[WS.slot_key[s]]
        WS.slot_key[s] = key
        WS.res[key] = s
        buf = ring[s]
        if key[0] == "w":
            j = key[1]
            P.dma("sp", None, lambda e: e.dma_start(out=buf.ap[:, :], in_=wsc[j]), key=f"ring{s}", reads=[wtok[j]], writes=[buf.t])
        else:
            j = key[1]
            P.dma("sp", None, lambda e: e.dma_start(out=buf.ap[:, 0:KW * 128], in_=dsc[j]), key=f"ring{s}", reads=[dtok[j]], writes=[buf.t])
        return buf

    def wblk(l, j):
        buf = wget(("w", l * NBLK + j))
        return buf, buf.ap[:, :].rearrange("p (k c) -> p k c", c=512)

    def dblk(l, f):
        buf = wget(("d", l * 8 + f))
        return buf, buf.ap[:, 0:KW * 128].rearrange("p (j c) -> p j c", c=128)

    def wsrc(l, j):
        if j < BLK_KV:
            return w_in[l][:, j * 512:(j + 1) * 512]
        if j < BLK_BR:
            jj = j - BLK_KV
            return w_kv[l][:, jj * 512:(jj + 1) * 512]
        if j < BLK_OUT:
            jj = j - BLK_BR
            return w_br[l][jj // 2][:, (jj % 2) * 512:(jj % 2 + 1) * 512]
        jj = j - BLK_OUT
        return w_out[l][:, jj * 512:(jj + 1) * 512]

    order = [(l, j) for l in range(2) for j in range(BLK_KV, BLK_BR)]
    rest = list(range(6, 10)) + list(range(0, 6)) + list(range(10, 22)) + list(range(26, 34))
    order += [(0, j) for j in rest] + [(1, j) for j in range(6, 10)]
    order2 = [(1, j) for j in rest if not 6 <= j < 10]
    cast_i = [0]

    def issue_casts(lst, extra_reads=()):
        for (l, j) in lst:
            i = cast_i[0]
            cast_i[0] += 1
            g = l * NBLK + j
            ck = P.tok(f"cchain{i % 8}")
            src = wsrc(l, j).rearrange("(k p) c -> p k c", p=128)
            dst = wsc[g].rearrange("p (k c) -> p k c", c=512)
            P.dma("pool", None, lambda e, src=src, dst=dst: e.dma_start(out=dst, in_=src), key=f"wc{i % 8}",
                  reads=[ck] + list(extra_reads), writes=[wtok[g], ck], nbytes=3 << 20)

    issue_casts(order)
    if not PACE:
        issue_casts(order2)

    def LOAD(buf_ap, src, key, writes):
        P.dma("act", None, lambda e: e.dma_start(out=buf_ap, in_=src), key=key, writes=writes, nbytes=1 << 19)

    LOAD(cst.ap[:, :], cst_in, "l_cst", [cst.t])
    LOAD(pc.ap[:, :], pc_in, "l_pc", [pc.t])
    LOAD(x_tm.ap[0:1, 2:4, :], bs_in.rearrange("l o n -> o l n"), "l_bs", [x_tm.s(2), x_tm.s(3)])
    for i, k in enumerate((0, 1, 4)):
        LOAD(gbn[i].ap[:, :], gb_in[k], f"l_gb{i}", [gbn[i].t])
    for q, k in enumerate((5, 6, 2, 3)):
        LOAD(yB4[:, q, :], gb_in[k], f"l_yb{q}", [yB.s(2 * q), yB.s(2 * q + 1)])
    wsf = []
    for l in range(2):
        for hh in range(2):
            tb = tmp.next()
            LOAD(tb.ap[:, :], wst_in[l][:, hh * 512:(hh + 1) * 512], f"l_ws{l}{hh}", [tb.t])
            wsf.append(tb)
    cw = tmp.next()
    LOAD(cw.ap[:, 0:2 * 8 * KW], cw_in, "l_cw", [cw.t])
    LOAD(x_tm.ap[:, 0:2, :], mem_in.rearrange("(c p) d -> p c d", p=128), "l_mem", [x_tm.s(0), x_tm.s(1)])

    P.op("dve", lambda e: e.memset(epsR.ap[:, 0:1], RMS_EPS), [], [epsR.t])
    P.op("dve", lambda e: e.memset(epsR.ap[:, 1:2], LN_EPS), [], [epsR.t])
    CP(ident_b.ap[:, :], ident_f, [cst.t], [ident_b.t])
    CP(ones_b.ap[:, :], ones_f, [cst.t], [ones_b.t])
    TS(pch.ap[:, :].rearrange("p (l q) -> p l q", l=2), pc.ap[:, :].rearrange("p (l q) -> p l q", l=2)[:, :, 16:32],
       0.5, 0.0, ALU.mult, ALU.add, [pc.t], [pch.t])

    def pcol(l, k, f):
        return pc.ap[:, l * 32 + k * 8 + f:l * 32 + k * 8 + f + 1]

    def pchcol(l, k, f):
        return pch.ap[:, l * 16 + k * 8 + f:l * 16 + k * 8 + f + 1]

    for l in range(2):
        for g in range(8):
            tb = wsf[2 * l + g // 4]
            src = tb.ap[:, (g % 4) * 128:(g % 4 + 1) * 128]
            TT(src, src, tril_f, ALU.mult, [tb.t, cst.t], [tb.t])
            CP(wsTb[l].ap[:, g, :], src, [tb.t], [wsTb[l].t])
            zp = zrot.next()
            MM(zp.ap[:, 0:128], yB4[:, l, g * 128:(g + 1) * 128], src, True, False,
               [yB.s(2 * l), yB.s(2 * l + 1), tb.t], [zp.t])
            MM(zp.ap[:, 0:128], cst.ap[0:1, 256:384], x_tm.ap[0:1, 2 + l, g * 128:(g + 1) * 128], False, True,
               [cst.t, x_tm.s(2 + l)], [zp.t])
            CP(biasA[l].ap[:, g, :], zp.ap[:, 0:128], [zp.t], [biasA[l].t])

    for l in range(2):
        for f in range(8):
            sbuf = ring[(l * 8 + f) % 2]
            dv = sbuf.ap[:, 0:KW * 128].rearrange("p (j c) -> p j c", c=128)
            base = (l * 8 + f) * KW
            in0 = ident_f.unsqueeze(1).to_broadcast([128, KW, 128])
            in1 = cw.ap[:, base:base + KW].unsqueeze(2).to_broadcast([128, KW, 128])
            TT(dv, in0, in1, ALU.mult, [cst.t, cw.t], [sbuf.t])
            j = l * 8 + f
            P.dma("sp", None, lambda e, sbuf=sbuf, j=j: e.dma_start(out=dsc[j], in_=sbuf.ap[:, 0:KW * 128]),
                  key=f"dst{j % 2}", reads=[sbuf.t], writes=[dtok[j]])

    OPT = {"pool": False}

    def peng():
        return "pool" if OPT["pool"] else "dve"

    def p0_stage1(c, gtile, xs):
        ga, gt = gtile
        k = 4 * c
        ACT(xs.ap[:, :], x_tm.ap[:, c, :], AF.Square, [x_tm.s(c)], [xs.t, smP.s(c)], accum=smP.ap[:, k:k + 1])
        if FSQ:
            ACT(smP.ap[:, k + 2:k + 3], smP.ap[:, k:k + 1], AF.Sqrt, [smP.s(c), epsR.t], [smP.s(c)], scale=1.0 / D, bias=epsR.ap[:, 0:1])
        else:
            TS(smP.ap[:, k + 1:k + 2], smP.ap[:, k:k + 1], 1.0 / D, RMS_EPS, ALU.mult, ALU.add, [smP.s(c)], [smP.s(c)])
            ACT(smP.ap[:, k + 2:k + 3], smP.ap[:, k + 1:k + 2], AF.Sqrt, [smP.s(c)], [smP.s(c)])
        RCP(smP.ap[:, k + 3:k + 4], smP.ap[:, k + 2:k + 3], [smP.s(c)], [smP.s(c)])
        STT(xs.ap[:, :], x_tm.ap[:, c, :], smP.ap[:, k + 3:k + 4], ga, ALU.mult, ALU.mult,
            [x_tm.s(c), smP.s(c)] + gt, [xs.t])

    def p0_stage2(c, xs):
        for f in range(8):
            P.op("pe", lambda e, xs=xs, f=f: e.transpose(out=tp.ap[:, f, :], in_=xs.ap[:, f * 128:(f + 1) * 128], identity=ident_b.ap[:, :]),
                 [xs.t, ident_b.t], [tp.s(f // 4)], cost=0.07)
        if os.environ.get("MK_ONEEVAC", "1") == "1":
            if c % 2 == 1 and os.environ.get("MK_EVAC_ALT", "1") == "1":
                CP(hT.ap[:, :, c * 128:(c + 1) * 128], tp.ap[:, :, :], [tp.s(0), tp.s(1)], [hT.s(0), hT.s(1)])
            else:
                ACT(hT.ap[:, :, c * 128:(c + 1) * 128], tp.ap[:, :, :], AF.Copy, [tp.s(0), tp.s(1)], [hT.s(0), hT.s(1)])
            return
        ACT(hT.ap[:, 0:4, c * 128:(c + 1) * 128], tp.ap[:, 0:4, :], AF.Copy, [tp.s(0)], [hT.s(0)])
        if SPLIT_EVAC:
            CP(hT.ap[:, 4:8, c * 128:(c + 1) * 128], tp.ap[:, 4:8, :], [tp.s(1)], [hT.s(1)])
        else:
            ACT(hT.ap[:, 4:8, c * 128:(c + 1) * 128], tp.ap[:, 4:8, :], AF.Copy, [tp.s(1)], [hT.s(1)])

    def rms_to_hT(nch, gtile):
        P.phase = "P0"
        pend = []
        for c in range(nch):
            xs = xs_rot.next()
            p0_stage1(c, gtile, xs)
            pend.append((c, xs))
            if len(pend) > 1:
                p0_stage2(*pend.pop(0))
        while pend:
            p0_stage2(*pend.pop(0))

    def ztile_fm(wv, wbuf, col0, rhs3, T, reads):
        zp = zrot.next()
        for kt in range(8):
            MM(zp.ap[:, 0:T], wv[:, kt, col0:col0 + 128], rhs3[:, kt, 0:T], kt == 0, kt == 7, [wbuf.t] + reads, [zp.t])
        return zp

    def silu2(zp, T):
        th = tmp.next()
        ACT(th.ap[:, 0:T], zp.ap[:, 0:T], AF.Tanh, [zp.t], [th.t], scale=0.5)
        STT(th.ap[:, 0:T], th.ap[:, 0:T], 1.0, zp.ap[:, 0:T], ALU.add, ALU.mult, [th.t, zp.t], [th.t])
        return th

    def phase_B1(l, nch):
        T = nch * 128
        P.phase = "B1"
        for f in range(8):
            wb, wv = wblk(l, BLK_BA + f // 4)
            zba = ztile_fm(wv, wb, (f % 4) * 128, hT.ap, T, hTt)
            wb, wv = wblk(l, BLK_BB + f // 4)
            zbb = ztile_fm(wv, wb, (f % 4) * 128, hT.ap, T, hTt)
            th = tmp.next()
            ACT(th.ap[:, 0:T], zbb.ap[:, 0:T], AF.Tanh, [zbb.t], [th.t], scale=0.5)
            STT(glu[l].ap[:, f, 30:30 + T], th.ap[:, 0:T], 1.0, zba.ap[:, 0:T], ALU.add, ALU.mult,
                [th.t, zba.t], [glu[l].s(f)])

    def halo_copy(l, nch):
        T = nch * 128
        toks = [glu[l].s(f) for f in range(8)]
        CP(glu[l].ap[:, :, 0:30], glu[l].ap[:, :, T:T + 30], toks, toks, eng="dve")

    def layer_body(l, nch):
        T = nch * 128
        P.phase = "A1"
        for c in range(nch):
            for h in range(2):
                wb, wv = wblk(l, BLK_V + h)
                zp = zrot.next()
                for kt in range(8):
                    MM(zp.ap[:, :], hT.ap[:, kt, c * 128:(c + 1) * 128], wv[:, kt, :], kt == 0, kt == 7, hTt + [wb.t], [zp.t])
                ACT(yB4[:, c, h * 512:(h + 1) * 512], zp.ap[:, :], AF.Gelu_apprx_tanh, [zp.t], [yB.s(2 * c + h)])
                P.op("dve", lambda e, c=c, h=h: e.bn_stats(out=st6.ap[:, h, :], in_=yB4[:, c, h * 512:(h + 1) * 512]),
                     [yB.s(2 * c + h)], [st6.t], cost=0.7)
            P.op("dve", lambda e, c=c: e.bn_aggr(out=mv.ap[:, c, :], in_=st6.ap[:, :, :]), [st6.t], [mv.t])
        TS(sm.ap[:, 16:16 + nch], mv.ap[:, 0:nch, 1], LN_EPS, 0.0, ALU.add, ALU.add, [mv.t], [sm.t])
        ACT(sm.ap[:, 20:20 + nch], sm.ap[:, 16:16 + nch], AF.Sqrt, [sm.t], [sm.t])
        RCP(sm.ap[:, 24:24 + nch], sm.ap[:, 20:20 + nch], [sm.t], [sm.t])
        STT(sm.ap[:, 28:28 + nch], mv.ap[:, 0:nch, 0], -1.0, sm.ap[:, 24:24 + nch], ALU.mult, ALU.mult, [mv.t, sm.t], [sm.t])
        for c in range(nch):
            TS(a_bf[:, c, :], yB4[:, c, :], sm.ap[:, 24 + c:25 + c], sm.ap[:, 28 + c:29 + c], ALU.mult, ALU.add,
               [yB.s(2 * c), yB.s(2 * c + 1), sm.t], [brC.s(2 * c), brC.s(2 * c + 1)])
        P.phase = "A2"
        for f in range(8):
            wb, wv = wblk(l, BLK_U + f // 4)
            zu = ztile_fm(wv, wb, (f % 4) * 128, hT.ap, T, hTt)
            ug = tmp.next()
            ACT(ug.ap[:, 0:T], zu.ap[:, 0:T], AF.Gelu_apprx_tanh, [zu.t], [ug.t])
            wb, wv = wblk(l, BLK_AG + f // 4)
            zag = ztile_fm(wv, wb, (f % 4) * 128, hT.ap, T, hTt)
            s2 = silu2(zag, T)
            sp = zrot.next()
            for c in range(nch):
                MM(sp.ap[:, c * 128:(c + 1) * 128], a_bf[:, c, f * 128:(f + 1) * 128], wsTb[l].ap[:, f, :], True, True,
                   [brC.s(2 * c + f // 4), wsTb[l].t], [sp.t])
            sv = tmp.next()
            STT(sv.ap[:, 0:T].rearrange("p (c t) -> p c t", t=128), sp.ap[:, 0:T].rearrange("p (c t) -> p c t", t=128),
                pcol(l, 0, f), biasA[l].ap[:, f, :].unsqueeze(1).to_broadcast([128, nch, 128]), ALU.mult, ALU.add,
                [sp.t, pc.t, biasA[l].t], [sv.t])
            TT(ug.ap[:, 0:T], ug.ap[:, 0:T], s2.ap[:, 0:T], ALU.mult, [ug.t, s2.t], [ug.t], eng=peng())
            TT(brA.ap[:, f, 0:T], sv.ap[:, 0:T], ug.ap[:, 0:T], ALU.mult, [sv.t, ug.t], [brA.s(f)])
        phase_B1(l, nch)
        P.phase = "B2"
        pend = []

        def stats_mm(f, ybf, ysq):
            MM(s1p.ap[:, 0:T], ones_b.ap[:, :], ybf.ap[:, 0:T], f == 0, f == 7, [ones_b.t, ybf.t], [s1p.t])
            MM(s2p.ap[:, 0:T], ones_b.ap[:, :], ysq.ap[:, 0:T], f == 0, f == 7, [ones_b.t, ysq.t], [s2p.t])

        for f in range(8):
            db, dv = dblk(l, f)
            yp = zrot.next()
            for j in range(KW):
                MM(yp.ap[:, 0:T], dv[:, j, :], glu[l].ap[:, f, j:j + T], j == 0, j == KW - 1, [db.t, glu[l].s(f)], [yp.t])
            if pend:
                stats_mm(*pend.pop(0))
            ACT(yB3[:, f, 0:T], yp.ap[:, 0:T], AF.Identity, [yp.t, pc.t], [yB.s(f)], scale=0.5, bias=pcol(l, 1, f))
            ysq = ysq_rot.next()
            ACT(ysq.ap[:, 0:T], yp.ap[:, 0:T], AF.Square, [yp.t, pc.t], [ysq.t], scale=0.5, bias=pcol(l, 1, f))
            ybf = ybf_rot.next()
            CP(ybf.ap[:, 0:T], yB3[:, f, 0:T], [yB.s(f)], [ybf.t])
            pend.append((f, ybf, ysq))
        halo_copy(l, nch)
        def phase_B3():
            P.phase = "B3"
            TS(meanT.ap[:, 0:T], s1p.ap[:, 0:T], 1.0 / D, 0.0, ALU.mult, ALU.add, [s1p.t], [meanT.t])
            msq = tmp.next()
            TT(msq.ap[:, 0:T], meanT.ap[:, 0:T], meanT.ap[:, 0:T], ALU.mult, [meanT.t], [msq.t])
            var = tmp.next()
            STT(var.ap[:, 0:T], s2p.ap[:, 0:T], 1.0 / D, msq.ap[:, 0:T], ALU.mult, ALU.subtract, [s2p.t, msq.t], [var.t])
            if FSQ2:
                ACT(msq.ap[:, 0:T], var.ap[:, 0:T], AF.Sqrt, [var.t, epsR.t], [msq.t], bias=epsR.ap[:, 1:2])
            else:
                TS(var.ap[:, 0:T], var.ap[:, 0:T], LN_EPS, 0.0, ALU.add, ALU.add, [var.t], [var.t])
                ACT(msq.ap[:, 0:T], var.ap[:, 0:T], AF.Sqrt, [var.t], [msq.t])
            RCP(rstdT.ap[:, 0:T], msq.ap[:, 0:T], [msq.t], [rstdT.t])

        P.phase = "C1"
        for f in range(8):
            wb, wv = wblk(l, BLK_CQ + f // 4)
            zq = ztile_fm(wv, wb, (f % 4) * 128, hT.ap, T, hTt)
            if f == 0:
                stats_mm(*pend.pop(0))
            CP(qm.ap[:, f, 0:T], zq.ap[:, 0:T], [zq.t], [qm.s(f)])
            if (f == 0 and B3EARLY) or (f == 7 and not B3EARLY):
                phase_B3()
                P.phase = "C1"
        def b4_item(f):
            st = {}

            def s0():
                P.phase = "B4"
                wb, wv = wblk(l, BLK_BG + f // 4)
                zbg = ztile_fm(wv, wb, (f % 4) * 128, hT.ap, T, hTt)
                s2 = s2rot.next()
                ACT(s2.ap[:, 0:T], zbg.ap[:, 0:T], AF.Tanh, [zbg.t], [s2.t], scale=0.5)
                STT(s2.ap[:, 0:T], s2.ap[:, 0:T], 1.0, zbg.ap[:, 0:T], ALU.add, ALU.mult, [s2.t, zbg.t], [s2.t])
                e_ = tmp.next()
                TT(e_.ap[:, 0:T], yB3[:, f, 0:T], meanT.ap[:, 0:T], ALU.subtract, [yB.s(f), meanT.t], [e_.t],
                   eng=(peng() if f % 2 == 0 else "dve"))
                TT(e_.ap[:, 0:T], e_.ap[:, 0:T], rstdT.ap[:, 0:T], ALU.mult, [e_.t, rstdT.t], [e_.t])
                st.update(s2=s2, e=e_)

            def s1():
                P.phase = "B4"
                e_ = st["e"]
                wq = tmp.next()
                ACT(wq.ap[:, 0:T], e_.ap[:, 0:T], AF.Identity, [e_.t, pch.t], [wq.t], scale=pchcol(l, 0, f), bias=pchcol(l, 1, f))
                ACT(e_.ap[:, 0:T], e_.ap[:, 0:T], AF.Tanh, [e_.t, pch.t], [e_.t], scale=pchcol(l, 0, f), bias=pchcol(l, 1, f))
                st.update(wq=wq)

            def s2_():
                P.phase = "B4"
                e_, wq, s2 = st["e"], st["wq"], st["s2"]
                STT(wq.ap[:, 0:T], e_.ap[:, 0:T], 1.0, wq.ap[:, 0:T], ALU.add, ALU.mult, [e_.t, wq.t], [wq.t])
                TT(brB.ap[:, f, 0:T], wq.ap[:, 0:T], s2.ap[:, 0:T], ALU.mult, [wq.t, s2.t], [brB.s(f)], eng=peng())

            return [s0, s1, s2_]

        def c2_item(hh):
            st = {}

            def s0():
                P.phase = "C2"
                PT = pt_rot.next()
                for mt in range(2):
                    sp = zrot.next()
                    for kk in range(2):
                        MM(sp.ap[:, 0:T], KT[l].ap[:, 2 * hh + kk, mt * 128:(mt + 1) * 128], qm.ap[:, 2 * hh + kk, 0:T], kk == 0, kk == 1,
                           [KT[l].t, qm.s(2 * hh + kk)], [sp.t])
                    ACT(PT.ap[:, mt, 0:T], sp.ap[:, 0:T], AF.Exp, [sp.t], [PT.t], scale=1.0 / 16.0)
                st.update(PT=PT)

            def s1():
                P.phase = "C2"
                PT = st["PT"]
                dp = zrot.next()
                for mt in range(2):
                    MM(dp.ap[:, 0:T], ones_b.ap[:, :], PT.ap[:, mt, 0:T], mt == 0, mt == 1, [ones_b.t, PT.t], [dp.t])
                rden = tmp.next()
                RCP(rden.ap[:, 0:T], dp.ap[:, 0:T], [dp.t], [rden.t])
                ops_, s2s = [], []
                for kk in range(2):
                    f = 2 * hh + kk
                    op_ = zrot.next()
                    for mt in range(2):
                        MM(op_.ap[:, 0:T], VV[l].ap[:, mt, f * 128:(f + 1) * 128], PT.ap[:, mt, 0:T], mt == 0, mt == 1, [VV[l].t, PT.t], [op_.t])
                    o_ = tmp.next()
                    TT(o_.ap[:, 0:T], op_.ap[:, 0:T], rden.ap[:, 0:T], ALU.mult, [op_.t, rden.t], [o_.t])
                    ops_.append(o_)
                for kk in range(2):
                    f = 2 * hh + kk
                    wb, wv = wblk(l, BLK_CG + f // 4)
                    zcg = ztile_fm(wv, wb, (f % 4) * 128, hT.ap, T, hTt)
                    s2 = s2rot.next()
                    ACT(s2.ap[:, 0:T], zcg.ap[:, 0:T], AF.Tanh, [zcg.t], [s2.t], scale=0.5)
                    STT(s2.ap[:, 0:T], s2.ap[:, 0:T], 1.0, zcg.ap[:, 0:T], ALU.add, ALU.mult, [s2.t, zcg.t], [s2.t])
                    s2s.append(s2)
                st.update(o=ops_, s2=s2s)

            def s2_():
                P.phase = "C2"
                for kk in range(2):
                    f = 2 * hh + kk
                    TT(brC3[:, f, 0:T], st["o"][kk].ap[:, 0:T], st["s2"][kk].ap[:, 0:T], ALU.mult,
                       [st["o"][kk].t, st["s2"][kk].t], [brC.s(f)], eng=peng())

            return [s0, s1, s2_]

        items = []
        for i in range(4):
            items += [b4_item(2 * i), b4_item(2 * i + 1), c2_item(i)]
        nst = 3
        for step in range(len(items) + nst - 1):
            for sg in reversed(range(nst)):
                it = step - sg
                if 0 <= it < len(items):
                    items[it][sg]()
        P.phase = "M"
        brs = [(brA.ap, [brA.s(f) for f in range(8)]), (brB.ap, [brB.s(f) for f in range(8)]), (brC3, [brC.s(f) for f in range(8)])]
        for half in range(2):
            for n in range(3):
                wbb, wbv = wblk(l, BLK_BR + n * 2 + half)
                wgb, wgv = wblk(l, BLK_MG + n * 2 + half)
                for f4 in range(4):
                    f = half * 4 + f4
                    pp = ztile_fm(wbv, wbb, f4 * 128, brs[n][0], T, brs[n][1])
                    gp = ztile_fm(wgv, wgb, f4 * 128, hT.ap, T, hTt)
                    th = tmp.next()
                    ACT(th.ap[:, 0:T], gp.ap[:, 0:T], AF.Tanh, [gp.t], [th.t], scale=0.5)
                    if n == 0:
                        STT(macc[f4].ap[:, 0:T], th.ap[:, 0:T], 1.0, pp.ap[:, 0:T], ALU.add, ALU.mult, [th.t, pp.t], [macc[f4].t])
                    else:
                        STT(th.ap[:, 0:T], th.ap[:, 0:T], 1.0, pp.ap[:, 0:T], ALU.add, ALU.mult, [th.t, pp.t], [th.t])
                        if n == 1:
                            TT(macc[f4].ap[:, 0:T], macc[f4].ap[:, 0:T], th.ap[:, 0:T], ALU.add, [macc[f4].t, th.t], [macc[f4].t], eng=peng())
                        else:
                            TT(qm.ap[:, f, 0:T], macc[f4].ap[:, 0:T], th.ap[:, 0:T], ALU.add, [macc[f4].t, th.t], [qm.s(f)], eng=peng())

    def fin_chunk(c, tile_idx):
        P.phase = "FIN"
        k = 4 * c
        ACT(yB4[:, c, :], x_tm.ap[:, c, :], AF.Square, [x_tm.s(c)], [yB.s(2 * c), yB.s(2 * c + 1), smF.s(c)], accum=smF.ap[:, k:k + 1])
        if FSQ:
            ACT(smF.ap[:, k + 2:k + 3], smF.ap[:, k:k + 1], AF.Sqrt, [smF.s(c), epsR.t], [smF.s(c)], scale=1.0 / D, bias=epsR.ap[:, 0:1])
        else:
            TS(smF.ap[:, k + 1:k + 2], smF.ap[:, k:k + 1], 1.0 / D, RMS_EPS, ALU.mult, ALU.add, [smF.s(c)], [smF.s(c)])
            ACT(smF.ap[:, k + 2:k + 3], smF.ap[:, k + 1:k + 2], AF.Sqrt, [smF.s(c)], [smF.s(c)])
        RCP(smF.ap[:, k + 3:k + 4], smF.ap[:, k + 2:k + 3], [smF.s(c)], [smF.s(c)])
        STT(yB4[:, c, :], x_tm.ap[:, c, :], smF.ap[:, k + 3:k + 4], gbn[2].ap[:, :], ALU.mult, ALU.mult,
            [x_tm.s(c), smF.s(c), gbn[2].t], [yB.s(2 * c), yB.s(2 * c + 1)])
        r0 = tile_idx * 512 + c * 128
        P.dma("sp", None, lambda e: e.dma_start(out=y_out[r0:r0 + 128, :], in_=yB4[:, c, :]), key=f"st{c}",
              reads=[yB.s(2 * c), yB.s(2 * c + 1)], writes=[P.tok(f"yout{c}")], final=True, nbytes=1 << 19)
        if tile_idx + 1 < 8:
            load_x_chunk(2 + 4 * (tile_idx + 1) + c, c)

    def o_phase(l, nch, nxt, tile_idx=None):
        qtoks = [qm.s(f) for f in range(8)]
        pend = []
        for c in range(nch):
            P.phase = "O"
            for h in range(2):
                wb, wv = wblk(l, BLK_OUT + h)
                zp = zrot.next()
                for kt in range(8):
                    MM(zp.ap[:, :], qm.ap[:, kt, c * 128:(c + 1) * 128], wv[:, kt, :], kt == 0, kt == 7, qtoks + [wb.t], [zp.t])
                STT(x_tm.ap[:, c, h * 512:(h + 1) * 512], zp.ap[:, :], 0.25, x_tm.ap[:, c, h * 512:(h + 1) * 512], ALU.mult, ALU.add,
                    [zp.t, x_tm.s(c)], [x_tm.s(c)])
            if nxt[0] == "p0":
                P.phase = "P0"
                xs = xs_rot.next()
                p0_stage1(c, nxt[1], xs)
                pend.append((c, xs))
                if len(pend) > 2:
                    p0_stage2(*pend.pop(0))
            elif nxt[0] == "fin":
                fin_chunk(c, tile_idx)
        P.phase = "P0"
        while pend:
            p0_stage2(*pend.pop(0))

    for l in range(2):
        rms_to_hT(2, (yB4[:, 2 + l, :], [yB.s(4 + 2 * l), yB.s(5 + 2 * l)]))
        P.phase = "KV"
        for f in range(8):
            wb, wv = wblk(l, BLK_KV + f // 4)
            zp = ztile_fm(wv, wb, (f % 4) * 128, hT.ap, MEM, hTt)
            CP(KT[l].ap[:, f, :], zp.ap[:, 0:MEM], [zp.t], [KT[l].t])
        for mt in range(2):
            for hb in range(2):
                wb, wv = wblk(l, BLK_KV + 2 + hb)
                zp = zrot.next()
                for kt in range(8):
                    MM(zp.ap[:, :], hT.ap[:, kt, mt * 128:(mt + 1) * 128], wv[:, kt, :], kt == 0, kt == 7, hTt + [wb.t], [zp.t])
                CP(VV[l].ap[:, mt, hb * 512:(hb + 1) * 512], zp.ap[:, :], [zp.t], [VV[l].t])
    for l in range(2):
        toks = [glu[l].s(f) for f in range(8)]
        P.op("dve", lambda e, l=l: e.memset(glu[l].ap[:, :, 0:30], 0.0), [], toks)

    def load_x_chunk(gc, c):
        P.dma("sp", None, lambda e: e.dma_start(out=x_tm.ap[:, c, :], in_=x_in[gc * 128:(gc + 1) * 128, :]), key=f"xl{c}",
              writes=[x_tm.s(c)], nbytes=1 << 19)

    g0 = (gbn[0].ap[:, :], [gbn[0].t])
    g1 = (gbn[1].ap[:, :], [gbn[1].t])
    load_x_chunk(0, 0)
    rms_to_hT(1, g0)
    phase_B1(0, 1)
    halo_copy(0, 1)
    load_x_chunk(1, 0)
    rms_to_hT(1, g0)
    layer_body(0, 1)
    o_phase(0, 1, ("p0", g1))
    phase_B1(1, 1)
    halo_copy(1, 1)
    for c in range(4):
        load_x_chunk(2 + c, c)
    for t in range(8):
        OPT["pool"] = t >= 1
        rms_to_hT(4, g0)
        if t == 0 and PACE:
            issue_casts(order2, extra_reads=hTt)
        layer_body(0, 4)
        o_phase(0, 4, ("p0", g1))
        layer_body(1, 4)
        o_phase(1, 4, ("fin",), tile_idx=t)
    if os.environ.get("MK_DEBUG_TAGS"):
        P.names = {}
    P.reorder = os.environ.get("MK_REORDER", "1") == "1"
    P.emit()
    if P.names is not None:
        with open(os.environ["MK_DEBUG_TAGS"], "w") as fh:
            json.dump(P.names, fh)
    return nc, P


def _prep_inputs(x, mem, norm_g, mem_norm_g, w_in, gmlp_ln_g, gmlp_ln_b, w_s, b_s, conv_w, conv_b,
                 conv_ln_g, conv_ln_b, w_kv, w_branch, w_out, final_norm_g):
    f32 = np.float32
    A = lambda a: np.ascontiguousarray(np.asarray(a, dtype=f32))
    x, mem = A(x), A(mem)
    bc = lambda v: np.broadcast_to(np.asarray(v, f32)[None, :], (128, D))
    gb = np.stack([bc(norm_g[0]), bc(norm_g[1]), bc(mem_norm_g[0]), bc(mem_norm_g[1]), bc(final_norm_g),
                   bc(gmlp_ln_b[0]), bc(gmlp_ln_b[1])]).astype(f32)
    fm = lambda v: np.asarray(v, f32).reshape(8, 128).T
    pc = np.zeros((128, 2, 4, 8), f32)
    for l in range(2):
        pc[:, l, 0] = fm(gmlp_ln_g[l])
        pc[:, l, 1] = fm(conv_b[l])
        pc[:, l, 2] = fm(conv_ln_g[l])
        pc[:, l, 3] = fm(conv_ln_b[l])
    cw = np.zeros((128, 2, 8, KW), f32)
    cwa = np.asarray(conv_w, f32)
    for l in range(2):
        cw[:, l] = cwa[l].reshape(KW, 8, 128).transpose(2, 1, 0)
    wst = np.asarray(w_s, f32).transpose(0, 3, 1, 2).reshape(2, 128, 1024)
    bs = np.asarray(b_s, f32).reshape(2, 1, 1024)
    cst = np.zeros((128, 384), f32)
    cst[:, 0:128] = np.eye(128, dtype=f32)
    cst[:, 128:256] = np.triu(np.ones((128, 128), f32))
    cst[:, 256:384] = 1.0
    shared = {
        "w_in": A(w_in), "w_kv": A(w_kv), "w_br": A(w_branch), "w_out": A(w_out),
        "gb_in": np.ascontiguousarray(gb), "pc_in": np.ascontiguousarray(pc.reshape(128, 64)),
        "cw_in": np.ascontiguousarray(cw.reshape(128, 2 * 8 * KW)), "wst_in": np.ascontiguousarray(wst),
        "bs_in": np.ascontiguousarray(bs), "cst_in": cst,
    }
    in_maps = []
    for i in range(NCORES):
        b, half = i // 2, i % 2
        xc = np.zeros((NCH_CORE * 128, D), f32)
        if half == 1:
            xc[:HALO] = x[b, TOK_CORE - HALO:TOK_CORE]
        xc[HALO:] = x[b, half * TOK_CORE:(half + 1) * TOK_CORE]
        m = dict(shared)
        m["x_in"] = xc
        m["mem_in"] = mem[b]
        in_maps.append(m)
    return in_maps


_CACHE = {}


def kernel(**inputs):
    in_maps = _prep_inputs(**inputs)
    if "nc" not in _CACHE:
        _CACHE["nc"] = build_program()[0]
    nc = _CACHE["nc"]
    res = run_bass_kernel_spmd(nc, in_maps, core_ids=list(range(NCORES)))
    out = np.empty((BATCH, SEQ, D), np.float32)
    for i in range(NCORES):
        b, half = i // 2, i % 2
        out[b, half * TOK_CORE:(half + 1) * TOK_CORE] = np.asarray(res.results[i]["y_out"], np.float32)
    return out
```

```python
import contextlib
import json
import os
import numpy as np
import concourse.bass as bass
import concourse.mybir as mybir
from concourse.bass_utils import run_bass_kernel_spmd

F32 = mybir.dt.float32
BF16 = mybir.dt.bfloat16
AF = mybir.ActivationFunctionType
ALU = mybir.AluOpType


class Tok:
    __slots__ = ("name", "lw", "rd")

    def __init__(self, name):
        self.name = name
        self.lw = None
        self.rd = []


class Buf:
    def __init__(self, ap, name):
        self.ap = ap
        self.name = name
        self.t = Tok(name)
        self._subs = {}

    def s(self, i):
        if i not in self._subs:
            self._subs[i] = Tok(f"{self.name}[{i}]")
        return self._subs[i]


class Op:
    __slots__ = ("idx", "eng", "emit", "dma_key", "deps", "sig", "final", "needs_sig", "tag", "cost", "nbytes",
                 "start", "finish", "pos")

    def __init__(self, idx, eng, emit, dma_key, final):
        self.idx = idx
        self.eng = eng
        self.emit = emit
        self.dma_key = dma_key
        self.deps = {}
        self.sig = None
        self.final = final
        self.needs_sig = False
        self.tag = ""
        self.cost = 0.3
        self.nbytes = 0
        self.start = 0.0
        self.finish = 0.0
        self.pos = 0


class Prog:
    ENGS = ("sp", "act", "dve", "pool", "pe")
    HOP = 0.12
    DMA_BW = 330e3
    DMA_LAT = 2.0
    WINDOW = 24

    def __init__(self, nc):
        self.nc = nc
        self.ops = []
        self.toks = {}
        self.stack = contextlib.ExitStack()
        self.phase = ""
        self.names = None
        self.reorder = True

    def sb(self, name, shape, dt):
        h = self.stack.enter_context(self.nc.sbuf_tensor(name, list(shape), dt))
        return Buf(h, name)

    def ps(self, name, shape, dt):
        h = self.stack.enter_context(self.nc.psum_tensor(name, list(shape), dt))
        return Buf(h, name)

    def tok(self, name):
        if name not in self.toks:
            self.toks[name] = Tok(name)
        return self.toks[name]

    def _record(self, eng, emit, reads, writes, dma_key=None, final=False, cost=0.3, nbytes=0):
        idx = len(self.ops)
        op = Op(idx, eng, emit, dma_key, final)
        op.tag = self.phase
        op.cost = cost
        op.nbytes = nbytes
        for t in reads:
            if t.lw is not None:
                op.deps[t.lw] = "raw"
        for t in writes:
            if t.lw is not None and t.lw not in op.deps:
                op.deps[t.lw] = "waw"
            for r in t.rd:
                if r not in op.deps and r != idx:
                    op.deps[r] = "war"
        for t in reads:
            t.rd.append(idx)
        for t in writes:
            t.rd = []
            t.lw = idx
        self.ops.append(op)
        return op

    def op(self, eng, emit, reads=(), writes=(), cost=0.3):
        return self._record(eng, emit, list(reads), list(writes), cost=cost)

    def dma(self, eng, _unused, emit, key, reads=(), writes=(), final=False, nbytes=1 << 20):
        return self._record(eng, emit, list(reads), list(writes), dma_key=key, final=final, cost=0.1, nbytes=nbytes)

    def _needs_sem(self, a, b, kind):
        if a.dma_key is not None or b.dma_key is not None:
            return True
        if a.eng != b.eng:
            return True
        if a.eng == "pe":
            return False
        return kind == "raw"

    def _schedule(self):
        ops = self.ops
        queues = {e: [o.idx for o in ops if o.eng == e] for e in self.ENGS}
        head = {e: 0 for e in self.ENGS}
        t_free = {e: 0.0 for e in self.ENGS}
        done = [False] * len(ops)
        dma_pipe = 0.0
        order = {e: [] for e in self.ENGS}
        remaining = len(ops)
        W = self.WINDOW
        while remaining:
            best = None
            for e in self.ENGS:
                q = queues[e]
                h = head[e]
                while h < len(q) and done[q[h]]:
                    h += 1
                head[e] = h
                if h >= len(q):
                    continue
                cnt = 0
                i = h
                tf = t_free[e]
                while i < len(q) and cnt < W:
                    oi = q[i]
                    i += 1
                    if done[oi]:
                        continue
                    cnt += 1
                    o = ops[oi]
                    st = tf
                    ok = True
                    for ai in o.deps:
                        if not done[ai]:
                            ok = False
                            break
                        a = ops[ai]
                        f = a.finish + (self.HOP if (a.eng != e or a.dma_key is not None) else 0.0)
                        if f > st:
                            st = f
                    if not ok:
                        continue
                    if best is None or st < best[0] - 1e-9 or (abs(st - best[0]) <= 1e-9 and oi < best[1]):
                        best = (st, oi, e)
                    if st <= tf + 1e-9:
                        break
            st, oi, e = best
            o = ops[oi]
            o.start = st
            if o.dma_key is not None:
                xfer0 = max(st + 0.1, dma_pipe)
                dma_pipe = xfer0 + o.nbytes / self.DMA_BW
                o.finish = dma_pipe + self.DMA_LAT
                t_free[e] = st + o.cost
            else:
                o.finish = st + o.cost
                t_free[e] = o.finish
            done[oi] = True
            order[e].append(oi)
            remaining -= 1
        self.sim_time = max(o.finish for o in ops)
        return order

    def emit(self):
        nc = self.nc
        ops = self.ops
        if self.reorder:
            order = self._schedule()
        else:
            order = {e: [o.idx for o in ops if o.eng == e] for e in self.ENGS}
        for e in self.ENGS:
            for p, oi in enumerate(order[e]):
                ops[oi].pos = p
        for b in ops:
            kept = {}
            latest = {}
            for ai, kind in b.deps.items():
                a = ops[ai]
                if not self._needs_sem(a, b, kind):
                    continue
                if a.dma_key is not None:
                    kept[ai] = kind
                elif a.eng not in latest or a.pos > ops[latest[a.eng]].pos:
                    latest[a.eng] = ai
            for ai in latest.values():
                kept[ai] = b.deps[ai]
            for ai in kept:
                ops[ai].needs_sig = True
            b.deps = kept
        cnt = {}
        keys = []
        for e in self.ENGS:
            for oi in order[e]:
                o = ops[oi]
                if o.dma_key is not None:
                    k = ("dma", o.dma_key)
                    cnt[k] = cnt.get(k, 0) + 16
                    o.sig = (k, cnt[k])
                elif o.needs_sig or o.final:
                    k = ("eng", o.eng)
                    cnt[k] = cnt.get(k, 0) + 1
                    o.sig = (k, cnt[k])
                else:
                    continue
                if k not in keys:
                    keys.append(k)
        sems = {}
        for k in keys:
            sems[k] = self.stack.enter_context(nc.semaphore(f"s_{k[0]}_{k[1]}".replace(" ", "")))
        self.n_sems = len(keys)
        finals = [o for o in ops if o.final]

        def run_stream(ename, e):
            seen = {}
            for oi in order[ename]:
                o = ops[oi]
                need = {}
                for ai in o.deps:
                    k, v = ops[ai].sig
                    if v > need.get(k, 0):
                        need[k] = v
                for k, v in need.items():
                    if seen.get(k, 0) >= v:
                        continue
                    e.wait_ge(sems[k], v)
                    seen[k] = v
                ins = o.emit(e)
                if self.names is not None:
                    self.names[ins.ins.name] = o.tag
                if o.sig is not None:
                    k, v = o.sig
                    ins.then_inc(sems[k], 16 if k[0] == "dma" else 1)
            if ename == "sp":
                for o in finals:
                    k, v = o.sig
                    if seen.get(k, 0) < v:
                        e.wait_ge(sems[k], v)
                        seen[k] = v

        with nc.Block() as block:
            @block.sync
            def _(e):
                run_stream("sp", e)

            @block.scalar
            def _(e):
                run_stream("act", e)

            @block.vector
            def _(e):
                run_stream("dve", e)

            @block.gpsimd
            def _(e):
                run_stream("pool", e)

            @block.tensor
            def _(e):
                run_stream("pe", e)
        self.stack.close()


D = 1024
NIN = 11264
SEQ = 8192
BATCH = 4
MEM = 256
NCORES = 8
TOK_CORE = 4096
HALO = 256
NCH_CORE = (TOK_CORE + HALO) // 128
KW = 31
RMS_EPS = 1e-6
LN_EPS = 1e-5
BLK_U, BLK_V, BLK_AG, BLK_BA, BLK_BB, BLK_BG, BLK_CQ, BLK_CG, BLK_MG = 0, 2, 4, 6, 8, 10, 12, 14, 16
BLK_KV, BLK_BR, BLK_OUT, NBLK = 22, 26, 32, 34
RING = 4
KDVE = int(os.environ.get('MK_KDVE', '8'))
FAST_RCP = False
B3EARLY = os.environ.get('MK_B3EARLY', '1') == '1'
PACE = os.environ.get('MK_PACE', '1') == '1'
SPLIT_EVAC = os.environ.get('MK_SPLIT', '1') == '1'
FSQ = os.environ.get('MK_FSQ', '1') == '1'
FSQ2 = os.environ.get('MK_FSQ2', '1') == '1'
NTMP = 8


class Rot:
    def __init__(self, bufs):
        self.bufs = bufs
        self.i = 0

    def next(self):
        b = self.bufs[self.i]
        self.i = (self.i + 1) % len(self.bufs)
        return b


def build_program():
    nc = bass.Bass("TRN2", target_bir_lowering=False)
    dt = nc.dram_tensor
    x_in = dt("x_in", [NCH_CORE * 128, D], F32, kind="ExternalInput").ap()
    mem_in = dt("mem_in", [MEM, D], F32, kind="ExternalInput").ap()
    w_in = dt("w_in", [2, D, NIN], F32, kind="ExternalInput").ap()
    w_kv = dt("w_kv", [2, D, 2 * D], F32, kind="ExternalInput").ap()
    w_br = dt("w_br", [2, 3, D, D], F32, kind="ExternalInput").ap()
    w_out = dt("w_out", [2, D, D], F32, kind="ExternalInput").ap()
    gb_in = dt("gb_in", [7, 128, D], F32, kind="ExternalInput").ap()
    pc_in = dt("pc_in", [128, 64], F32, kind="ExternalInput").ap()
    cw_in = dt("cw_in", [128, 2 * 8 * KW], F32, kind="ExternalInput").ap()
    wst_in = dt("wst_in", [2, 128, 1024], F32, kind="ExternalInput").ap()
    bs_in = dt("bs_in", [2, 1, 1024], F32, kind="ExternalInput").ap()
    cst_in = dt("cst_in", [128, 384], F32, kind="ExternalInput").ap()
    y_out = dt("y_out", [TOK_CORE, D], F32, kind="ExternalOutput").ap()
    wsc = dt("wsc", [2 * NBLK, 128, 4096], BF16, kind="Internal").ap()
    dsc = dt("dsc", [16, 128, KW * 128], BF16, kind="Internal").ap()

    P = Prog(nc)
    cst = P.sb("cst", [128, 384], F32)
    ident_b = P.sb("ident_b", [128, 128], BF16)
    ones_b = P.sb("ones_b", [128, 128], BF16)
    pc = P.sb("pc", [128, 64], F32)
    pch = P.sb("pch", [128, 32], F32)
    gbn = [P.sb(f"gbn{i}", [128, D], F32) for i in range(3)]
    KT = [P.sb(f"KT{l}", [128, 8, MEM], BF16) for l in range(2)]
    VV = [P.sb(f"VV{l}", [128, 2, D], BF16) for l in range(2)]
    wsTb = [P.sb(f"wsTb{l}", [128, 8, 128], BF16) for l in range(2)]
    biasA = [P.sb(f"biasA{l}", [128, 8, 128], F32) for l in range(2)]
    glu = [P.sb(f"glu{l}", [128, 8, 30 + 512], BF16) for l in range(2)]
    x_tm = P.sb("x_tm", [128, 4, D], F32)
    xs_rot = Rot([P.sb(f"xs{i}", [128, D], BF16) for i in range(3)])
    hT = P.sb("hT", [128, 8, 512], BF16)
    brA = P.sb("brA", [128, 8, 512], BF16)
    brB = P.sb("brB", [128, 8, 512], BF16)
    brC = P.sb("brC", [128, 4096], BF16)
    qm = P.sb("qm", [128, 8, 512], BF16)
    yB = P.sb("yB", [128, 4096], F32)
    ybf_rot = Rot([P.sb(f"ybf{i}", [128, 512], BF16) for i in range(2)])
    ysq_rot = Rot([P.sb(f"ysq{i}", [128, 512], BF16) for i in range(2)])
    pt_rot = Rot([P.sb(f"PT{i}", [128, 2, 512], BF16) for i in range(2)])
    tmp = Rot([P.sb(f"tmp{i}", [128, 512], F32) for i in range(NTMP)])
    macc = [P.sb(f"macc{i}", [128, 512], F32) for i in range(4)]
    s2rot = Rot(macc)
    meanT = P.sb("meanT", [128, 512], F32)
    rstdT = P.sb("rstdT", [128, 512], F32)
    ring = [P.sb(f"ring{i}", [128, 4096], BF16) for i in range(RING)]
    sm = P.sb("sm", [128, 32], F32)
    smP = P.sb("smP", [128, 16], F32)
    smF = P.sb("smF", [128, 16], F32)
    epsR = P.sb("epsR", [128, 2], F32)
    cwp = P.sb("cwp", [128, 2 * 8 * KW], F32) if KDVE > 0 else None
    st6 = P.sb("st6", [128, 2, 6], F32)
    mv = P.sb("mv", [128, 4, 2], F32)
    zrot = Rot([P.ps(f"zp{i}", [128, 512], F32) for i in range(5)])
    tp = P.ps("tp", [128, 8, 128], BF16)
    s1p = P.ps("s1p", [128, 512], F32)
    s2p = P.ps("s2p", [128, 512], F32)

    hTt = [hT.s(0), hT.s(1)]
    a_bf = brC.ap[:, :].rearrange("p (c d) -> p c d", d=D)
    brC3 = brC.ap[:, :].rearrange("p (f t) -> p f t", t=512)
    yB3 = yB.ap[:, :].rearrange("p (f t) -> p f t", t=512)
    yB4 = yB.ap[:, :].rearrange("p (c d) -> p c d", d=D)
    ident_f = cst.ap[:, 0:128]
    tril_f = cst.ap[:, 128:256]
    ones_f = cst.ap[:, 256:384]

    def fs(ap):
        n = 1
        for d in ap.shape[1:]:
            n *= d
        return n

    def ACT(out, in_, func, reads, writes, scale=1.0, bias=None, accum=None):
        kw = {}
        if bias is not None:
            kw["bias"] = bias
        if accum is not None:
            kw["accum_out"] = accum
        return P.op("act", lambda e: e.activation(out=out, in_=in_, func=func, scale=scale, **kw), reads, writes,
                    cost=0.25 + fs(out) / 1200.0)

    def vcost(eng, out, fast=False):
        n = fs(out)
        if eng == "pool":
            return 0.2 + n * 0.0021
        return 0.16 + n / (1920.0 if fast else 960.0)

    def TT(out, in0, in1, op, reads, writes, eng="dve"):
        return P.op(eng, lambda e: e.tensor_tensor(out=out, in0=in0, in1=in1, op=op), reads, writes, cost=vcost(eng, out))

    def STT(out, in0, scalar, in1, op0, op1, reads, writes, eng="dve"):
        return P.op(eng, lambda e: e.scalar_tensor_tensor(out=out, in0=in0, scalar=scalar, in1=in1, op0=op0, op1=op1), reads, writes,
                    cost=vcost(eng, out))

    def TS(out, in0, s1, s2, op0, op1, reads, writes, eng="dve"):
        return P.op(eng, lambda e: e.tensor_scalar(out=out, in0=in0, scalar1=s1, scalar2=s2, op0=op0, op1=op1), reads, writes,
                    cost=vcost(eng, out, fast=True))

    def CP(out, in_, reads, writes, eng="dve"):
        return P.op(eng, lambda e: e.tensor_copy(out=out, in_=in_), reads, writes, cost=vcost(eng, out))

    def RCP(out, in_, reads, writes):
        if fs(out) >= 64 and FAST_RCP:
            return P.op("dve", lambda e: e.reciprocal_approx_fast(out=out, in_=in_), reads, writes, cost=0.16 + fs(out) * 0.0013)
        return P.op("dve", lambda e: e.reciprocal(out=out, in_=in_), reads, writes, cost=0.1 + fs(out) * 0.0065)

    def MM(out, lhsT, rhs, start, stop, reads, writes):
        return P.op("pe", lambda e: e.matmul(out, lhsT=lhsT, rhs=rhs, start=start, stop=stop), reads, writes,
                    cost=0.008 + max(fs(out), 64) / 2400.0)

    wtok = [P.tok(f"wsc{j}") for j in range(2 * NBLK)]
    dtok = [P.tok(f"dsc{j}") for j in range(16)]

    class WS:
        res = {}
        slot_key = [None] * RING
        nxt = 0

    def wget(key):
        if key in WS.res:
            return ring[WS.res[key]]
        s = WS.nxt
        WS.nxt = (s + 1) % RING
        if WS.slot_key[s] is not None:
            del WS.res[WS.slot_key[s]]
        WS.slot_key[s] = key
        WS.res[key] = s
        buf = ring[s]
        if key[0] == "w":
            j = key[1]
            P.dma("sp", None, lambda e: e.dma_start(out=buf.ap[:, :], in_=wsc[j]), key=f"ring{s}", reads=[wtok[j]], writes=[buf.t])
        else:
            j = key[1]
            P.dma("sp", None, lambda e: e.dma_start(out=buf.ap[:, 0:KW * 128], in_=dsc[j]), key=f"ring{s}", reads=[dtok[j]], writes=[buf.t])
        return buf

    def wblk(l, j):
        buf = wget(("w", l * NBLK + j))
        return buf, buf.ap[:, :].rearrange("p (k c) -> p k c", c=512)

    def dblk(l, f):
        buf = wget(("d", l * 8 + f))
        return buf, buf.ap[:, 0:KW * 128].rearrange("p (j c) -> p j c", c=128)

    def wsrc(l, j):
        if j < BLK_KV:
            return w_in[l][:, j * 512:(j + 1) * 512]
        if j < BLK_BR:
            jj = j - BLK_KV
            return w_kv[l][:, jj * 512:(jj + 1) * 512]
        if j < BLK_OUT:
            jj = j - BLK_BR
            return w_br[l][jj // 2][:, (jj % 2) * 512:(jj % 2 + 1) * 512]
        jj = j - BLK_OUT
        return w_out[l][:, jj * 512:(jj + 1) * 512]

    order = [(l, j) for l in range(2) for j in range(BLK_KV, BLK_BR)]
    rest = list(range(6, 10)) + list(range(0, 6)) + list(range(10, 22)) + list(range(26, 34))
    order += [(0, j) for j in rest] + [(1, j) for j in range(6, 10)]
    order2 = [(1, j) for j in rest if not 6 <= j < 10]
    cast_i = [0]

    def issue_casts(lst, extra_reads=()):
        for (l, j) in lst:
            i = cast_i[0]
            cast_i[0] += 1
            g = l * NBLK + j
            ck = P.tok(f"cchain{i % 8}")
            src = wsrc(l, j).rearrange("(k p) c -> p k c", p=128)
            dst = wsc[g].rearrange("p (k c) -> p k c", c=512)
            P.dma("pool", None, lambda e, src=src, dst=dst: e.dma_start(out=dst, in_=src), key=f"wc{i % 8}",
                  reads=[ck] + list(extra_reads), writes=[wtok[g], ck], nbytes=3 << 20)

    issue_casts(order)
    if not PACE:
        issue_casts(order2)

    def LOAD(buf_ap, src, key, writes):
        P.dma("act", None, lambda e: e.dma_start(out=buf_ap, in_=src), key=key, writes=writes, nbytes=1 << 19)

    LOAD(cst.ap[:, :], cst_in, "l_cst", [cst.t])
    LOAD(pc.ap[:, :], pc_in, "l_pc", [pc.t])
    LOAD(x_tm.ap[0:1, 2:4, :], bs_in.rearrange("l o n -> o l n"), "l_bs", [x_tm.s(2), x_tm.s(3)])
    for i, k in enumerate((0, 1, 4)):
        LOAD(gbn[i].ap[:, :], gb_in[k], f"l_gb{i}", [gbn[i].t])
    for q, k in enumerate((5, 6, 2, 3)):
        LOAD(yB4[:, q, :], gb_in[k], f"l_yb{q}", [yB.s(2 * q), yB.s(2 * q + 1)])
    wsf = []
    for l in range(2):
        for hh in range(2):
            tb = tmp.next()
            LOAD(tb.ap[:, :], wst_in[l][:, hh * 512:(hh + 1) * 512], f"l_ws{l}{hh}", [tb.t])
            wsf.append(tb)
    cw = tmp.next()
    LOAD(cw.ap[:, 0:2 * 8 * KW], cw_in, "l_cw", [cw.t])
    if cwp is not None:
        LOAD(cwp.ap[:, :], cw_in, "l_cwp", [cwp.t])
    LOAD(x_tm.ap[:, 0:2, :], mem_in.rearrange("(c p) d -> p c d", p=128), "l_mem", [x_tm.s(0), x_tm.s(1)])

    P.op("dve", lambda e: e.memset(epsR.ap[:, 0:1], RMS_EPS), [], [epsR.t])
    P.op("dve", lambda e: e.memset(epsR.ap[:, 1:2], LN_EPS), [], [epsR.t])
    CP(ident_b.ap[:, :], ident_f, [cst.t], [ident_b.t])
    CP(ones_b.ap[:, :], ones_f, [cst.t], [ones_b.t])
    TS(pch.ap[:, :].rearrange("p (l q) -> p l q", l=2), pc.ap[:, :].rearrange("p (l q) -> p l q", l=2)[:, :, 16:32],
       0.5, 0.0, ALU.mult, ALU.add, [pc.t], [pch.t])

    def pcol(l, k, f):
        return pc.ap[:, l * 32 + k * 8 + f:l * 32 + k * 8 + f + 1]

    def pchcol(l, k, f):
        return pch.ap[:, l * 16 + k * 8 + f:l * 16 + k * 8 + f + 1]

    for l in range(2):
        for g in range(8):
            tb = wsf[2 * l + g // 4]
            src = tb.ap[:, (g % 4) * 128:(g % 4 + 1) * 128]
            TT(src, src, tril_f, ALU.mult, [tb.t, cst.t], [tb.t])
            CP(wsTb[l].ap[:, g, :], src, [tb.t], [wsTb[l].t])
            zp = zrot.next()
            MM(zp.ap[:, 0:128], yB4[:, l, g * 128:(g + 1) * 128], src, True, False,
               [yB.s(2 * l), yB.s(2 * l + 1), tb.t], [zp.t])
            MM(zp.ap[:, 0:128], cst.ap[0:1, 256:384], x_tm.ap[0:1, 2 + l, g * 128:(g + 1) * 128], False, True,
               [cst.t, x_tm.s(2 + l)], [zp.t])
            CP(biasA[l].ap[:, g, :], zp.ap[:, 0:128], [zp.t], [biasA[l].t])

    for l in range(2):
        for f in range(8):
            sbuf = ring[(l * 8 + f) % 2]
            dv = sbuf.ap[:, 0:KW * 128].rearrange("p (j c) -> p j c", c=128)
            base = (l * 8 + f) * KW
            in0 = ident_f.unsqueeze(1).to_broadcast([128, KW, 128])
            in1 = cw.ap[:, base:base + KW].unsqueeze(2).to_broadcast([128, KW, 128])
            TT(dv, in0, in1, ALU.mult, [cst.t, cw.t], [sbuf.t])
            j = l * 8 + f
            P.dma("sp", None, lambda e, sbuf=sbuf, j=j: e.dma_start(out=dsc[j], in_=sbuf.ap[:, 0:KW * 128]),
                  key=f"dst{j % 2}", reads=[sbuf.t], writes=[dtok[j]])

    OPT = {"pool": False}

    def peng():
        return "pool" if OPT["pool"] else "dve"

    def p0_stage1(c, gtile, xs):
        ga, gt = gtile
        k = 4 * c
        ACT(xs.ap[:, :], x_tm.ap[:, c, :], AF.Square, [x_tm.s(c)], [xs.t, smP.s(c)], accum=smP.ap[:, k:k + 1])
        if FSQ:
            ACT(smP.ap[:, k + 2:k + 3], smP.ap[:, k:k + 1], AF.Sqrt, [smP.s(c), epsR.t], [smP.s(c)], scale=1.0 / D, bias=epsR.ap[:, 0:1])
        else:
            TS(smP.ap[:, k + 1:k + 2], smP.ap[:, k:k + 1], 1.0 / D, RMS_EPS, ALU.mult, ALU.add, [smP.s(c)], [smP.s(c)])
            ACT(smP.ap[:, k + 2:k + 3], smP.ap[:, k + 1:k + 2], AF.Sqrt, [smP.s(c)], [smP.s(c)])
        RCP(smP.ap[:, k + 3:k + 4], smP.ap[:, k + 2:k + 3], [smP.s(c)], [smP.s(c)])
        STT(xs.ap[:, :], x_tm.ap[:, c, :], smP.ap[:, k + 3:k + 4], ga, ALU.mult, ALU.mult,
            [x_tm.s(c), smP.s(c)] + gt, [xs.t])

    def p0_stage2(c, xs):
        for f in range(8):
            P.op("pe", lambda e, xs=xs, f=f: e.transpose(out=tp.ap[:, f, :], in_=xs.ap[:, f * 128:(f + 1) * 128], identity=ident_b.ap[:, :]),
                 [xs.t, ident_b.t], [tp.s(f // 4)], cost=0.07)
        if os.environ.get("MK_ONEEVAC", "1") == "1":
            if c % 2 == 1 and os.environ.get("MK_EVAC_ALT", "1") == "1":
                CP(hT.ap[:, :, c * 128:(c + 1) * 128], tp.ap[:, :, :], [tp.s(0), tp.s(1)], [hT.s(0), hT.s(1)])
            else:
                ACT(hT.ap[:, :, c * 128:(c + 1) * 128], tp.ap[:, :, :], AF.Copy, [tp.s(0), tp.s(1)], [hT.s(0), hT.s(1)])
            return
        ACT(hT.ap[:, 0:4, c * 128:(c + 1) * 128], tp.ap[:, 0:4, :], AF.Copy, [tp.s(0)], [hT.s(0)])
        if SPLIT_EVAC:
            CP(hT.ap[:, 4:8, c * 128:(c + 1) * 128], tp.ap[:, 4:8, :], [tp.s(1)], [hT.s(1)])
        else:
            ACT(hT.ap[:, 4:8, c * 128:(c + 1) * 128], tp.ap[:, 4:8, :], AF.Copy, [tp.s(1)], [hT.s(1)])

    def rms_to_hT(nch, gtile):
        P.phase = "P0"
        pend = []
        for c in range(nch):
            xs = xs_rot.next()
            p0_stage1(c, gtile, xs)
            pend.append((c, xs))
            if len(pend) > 1:
                p0_stage2(*pend.pop(0))
        while pend:
            p0_stage2(*pend.pop(0))

    def ztile_fm(wv, wbuf, col0, rhs3, T, reads):
        zp = zrot.next()
        for kt in range(8):
            MM(zp.ap[:, 0:T], wv[:, kt, col0:col0 + 128], rhs3[:, kt, 0:T], kt == 0, kt == 7, [wbuf.t] + reads, [zp.t])
        return zp

    def silu2(zp, T):
        th = tmp.next()
        ACT(th.ap[:, 0:T], zp.ap[:, 0:T], AF.Tanh, [zp.t], [th.t], scale=0.5)
        STT(th.ap[:, 0:T], th.ap[:, 0:T], 1.0, zp.ap[:, 0:T], ALU.add, ALU.mult, [th.t, zp.t], [th.t])
        return th

    def phase_B1(l, nch):
        T = nch * 128
        P.phase = "B1"
        for f in range(8):
            wb, wv = wblk(l, BLK_BA + f // 4)
            zba = ztile_fm(wv, wb, (f % 4) * 128, hT.ap, T, hTt)
            wb, wv = wblk(l, BLK_BB + f // 4)
            zbb = ztile_fm(wv, wb, (f % 4) * 128, hT.ap, T, hTt)
            th = tmp.next()
            ACT(th.ap[:, 0:T], zbb.ap[:, 0:T], AF.Tanh, [zbb.t], [th.t], scale=0.5)
            STT(glu[l].ap[:, f, 30:30 + T], th.ap[:, 0:T], 1.0, zba.ap[:, 0:T], ALU.add, ALU.mult,
                [th.t, zba.t], [glu[l].s(f)])

    def halo_copy(l, nch):
        T = nch * 128
        toks = [glu[l].s(f) for f in range(8)]
        CP(glu[l].ap[:, :, 0:30], glu[l].ap[:, :, T:T + 30], toks, toks, eng="dve")

    def layer_body(l, nch):
        T = nch * 128
        P.phase = "A1"
        for c in range(nch):
            for h in range(2):
                wb, wv = wblk(l, BLK_V + h)
                zp = zrot.next()
                for kt in range(8):
                    MM(zp.ap[:, :], hT.ap[:, kt, c * 128:(c + 1) * 128], wv[:, kt, :], kt == 0, kt == 7, hTt + [wb.t], [zp.t])
                ACT(yB4[:, c, h * 512:(h + 1) * 512], zp.ap[:, :], AF.Gelu_apprx_tanh, [zp.t], [yB.s(2 * c + h)])
                P.op("dve", lambda e, c=c, h=h: e.bn_stats(out=st6.ap[:, h, :], in_=yB4[:, c, h * 512:(h + 1) * 512]),
                     [yB.s(2 * c + h)], [st6.t], cost=0.7)
            P.op("dve", lambda e, c=c: e.bn_aggr(out=mv.ap[:, c, :], in_=st6.ap[:, :, :]), [st6.t], [mv.t])
        TS(sm.ap[:, 16:16 + nch], mv.ap[:, 0:nch, 1], LN_EPS, 0.0, ALU.add, ALU.add, [mv.t], [sm.t])
        ACT(sm.ap[:, 20:20 + nch], sm.ap[:, 16:16 + nch], AF.Sqrt, [sm.t], [sm.t])
        RCP(sm.ap[:, 24:24 + nch], sm.ap[:, 20:20 + nch], [sm.t], [sm.t])
        STT(sm.ap[:, 28:28 + nch], mv.ap[:, 0:nch, 0], -1.0, sm.ap[:, 24:24 + nch], ALU.mult, ALU.mult, [mv.t, sm.t], [sm.t])
        for c in range(nch):
            TS(a_bf[:, c, :], yB4[:, c, :], sm.ap[:, 24 + c:25 + c], sm.ap[:, 28 + c:29 + c], ALU.mult, ALU.add,
               [yB.s(2 * c), yB.s(2 * c + 1), sm.t], [brC.s(2 * c), brC.s(2 * c + 1)])
        P.phase = "A2"
        for f in range(8):
            wb, wv = wblk(l, BLK_U + f // 4)
            zu = ztile_fm(wv, wb, (f % 4) * 128, hT.ap, T, hTt)
            ug = tmp.next()
            ACT(ug.ap[:, 0:T], zu.ap[:, 0:T], AF.Gelu_apprx_tanh, [zu.t], [ug.t])
            wb, wv = wblk(l, BLK_AG + f // 4)
            zag = ztile_fm(wv, wb, (f % 4) * 128, hT.ap, T, hTt)
            s2 = silu2(zag, T)
            sp = zrot.next()
            for c in range(nch):
                MM(sp.ap[:, c * 128:(c + 1) * 128], a_bf[:, c, f * 128:(f + 1) * 128], wsTb[l].ap[:, f, :], True, True,
                   [brC.s(2 * c + f // 4), wsTb[l].t], [sp.t])
            sv = tmp.next()
            STT(sv.ap[:, 0:T].rearrange("p (c t) -> p c t", t=128), sp.ap[:, 0:T].rearrange("p (c t) -> p c t", t=128),
                pcol(l, 0, f), biasA[l].ap[:, f, :].unsqueeze(1).to_broadcast([128, nch, 128]), ALU.mult, ALU.add,
                [sp.t, pc.t, biasA[l].t], [sv.t])
            TT(ug.ap[:, 0:T], ug.ap[:, 0:T], s2.ap[:, 0:T], ALU.mult, [ug.t, s2.t], [ug.t], eng=peng())
            TT(brA.ap[:, f, 0:T], sv.ap[:, 0:T], ug.ap[:, 0:T], ALU.mult, [sv.t, ug.t], [brA.s(f)])
        phase_B1(l, nch)
        P.phase = "B2"
        pend = []

        def stats_mm(f, ybf, ysq):
            MM(s1p.ap[:, 0:T], ones_b.ap[:, :], ybf.ap[:, 0:T], f == 0, f == 7, [ones_b.t, ybf.t], [s1p.t])
            MM(s2p.ap[:, 0:T], ones_b.ap[:, :], ysq.ap[:, 0:T], f == 0, f == 7, [ones_b.t, ysq.t], [s2p.t])

        for f in range(8):
            db, dv = dblk(l, f)
            kp = KDVE
            acc = None
            if kp:
                acc = tmp.next()
                base = (l * 8 + f) * KW
                for j in range(kp):
                    wcol = cwp.ap[:, base + j:base + j + 1]
                    if j == 0:
                        TS(acc.ap[:, 0:T], glu[l].ap[:, f, j:j + T], wcol, 0.0, ALU.mult, ALU.add, [glu[l].s(f), cwp.t], [acc.t])
                    else:
                        STT(acc.ap[:, 0:T], glu[l].ap[:, f, j:j + T], wcol, acc.ap[:, 0:T], ALU.mult, ALU.add,
                            [glu[l].s(f), cwp.t, acc.t], [acc.t])
            yp = zrot.next()
            for j in range(kp, KW):
                MM(yp.ap[:, 0:T], dv[:, j, :], glu[l].ap[:, f, j:j + T], j == kp, j == KW - 1, [db.t, glu[l].s(f)], [yp.t])
            if pend:
                stats_mm(*pend.pop(0))
            ysq = ysq_rot.next()
            if kp:
                STT(yB3[:, f, 0:T], acc.ap[:, 0:T], 1.0, yp.ap[:, 0:T], ALU.mult, ALU.add, [acc.t, yp.t], [yB.s(f)])
                ACT(ysq.ap[:, 0:T], yB3[:, f, 0:T], AF.Square, [yB.s(f), pc.t], [ysq.t], scale=0.5, bias=pcol(l, 1, f))
                ACT(yB3[:, f, 0:T], yB3[:, f, 0:T], AF.Identity, [yB.s(f), pc.t], [yB.s(f)], scale=0.5, bias=pcol(l, 1, f))
            else:
                ACT(yB3[:, f, 0:T], yp.ap[:, 0:T], AF.Identity, [yp.t, pc.t], [yB.s(f)], scale=0.5, bias=pcol(l, 1, f))
                ACT(ysq.ap[:, 0:T], yp.ap[:, 0:T], AF.Square, [yp.t, pc.t], [ysq.t], scale=0.5, bias=pcol(l, 1, f))
            ybf = ybf_rot.next()
            CP(ybf.ap[:, 0:T], yB3[:, f, 0:T], [yB.s(f)], [ybf.t])
            pend.append((f, ybf, ysq))
        halo_copy(l, nch)
        def phase_B3():
            P.phase = "B3"
            TS(meanT.ap[:, 0:T], s1p.ap[:, 0:T], 1.0 / D, 0.0, ALU.mult, ALU.add, [s1p.t], [meanT.t])
            msq = tmp.next()
            TT(msq.ap[:, 0:T], meanT.ap[:, 0:T], meanT.ap[:, 0:T], ALU.mult, [meanT.t], [msq.t])
            var = tmp.next()
            STT(var.ap[:, 0:T], s2p.ap[:, 0:T], 1.0 / D, msq.ap[:, 0:T], ALU.mult, ALU.subtract, [s2p.t, msq.t], [var.t])
            if FSQ2:
                ACT(msq.ap[:, 0:T], var.ap[:, 0:T], AF.Sqrt, [var.t, epsR.t], [msq.t], bias=epsR.ap[:, 1:2])
            else:
                TS(var.ap[:, 0:T], var.ap[:, 0:T], LN_EPS, 0.0, ALU.add, ALU.add, [var.t], [var.t])
                ACT(msq.ap[:, 0:T], var.ap[:, 0:T], AF.Sqrt, [var.t], [msq.t])
            RCP(rstdT.ap[:, 0:T], msq.ap[:, 0:T], [msq.t], [rstdT.t])

        P.phase = "C1"
        for f in range(8):
            wb, wv = wblk(l, BLK_CQ + f // 4)
            zq = ztile_fm(wv, wb, (f % 4) * 128, hT.ap, T, hTt)
            if f == 0:
                stats_mm(*pend.pop(0))
            CP(qm.ap[:, f, 0:T], zq.ap[:, 0:T], [zq.t], [qm.s(f)])
            if (f == 0 and B3EARLY) or (f == 7 and not B3EARLY):
                phase_B3()
                P.phase = "C1"
        def b4_item(f):
            st = {}

            def s0():
                P.phase = "B4"
                wb, wv = wblk(l, BLK_BG + f // 4)
                zbg = ztile_fm(wv, wb, (f % 4) * 128, hT.ap, T, hTt)
                s2 = s2rot.next()
                ACT(s2.ap[:, 0:T], zbg.ap[:, 0:T], AF.Tanh, [zbg.t], [s2.t], scale=0.5)
                STT(s2.ap[:, 0:T], s2.ap[:, 0:T], 1.0, zbg.ap[:, 0:T], ALU.add, ALU.mult, [s2.t, zbg.t], [s2.t])
                e_ = tmp.next()
                TT(e_.ap[:, 0:T], yB3[:, f, 0:T], meanT.ap[:, 0:T], ALU.subtract, [yB.s(f), meanT.t], [e_.t],
                   eng=(peng() if f % 2 == 0 else "dve"))
                TT(e_.ap[:, 0:T], e_.ap[:, 0:T], rstdT.ap[:, 0:T], ALU.mult, [e_.t, rstdT.t], [e_.t])
                st.update(s2=s2, e=e_)

            def s1():
                P.phase = "B4"
                e_ = st["e"]
                wq = tmp.next()
                ACT(wq.ap[:, 0:T], e_.ap[:, 0:T], AF.Identity, [e_.t, pch.t], [wq.t], scale=pchcol(l, 0, f), bias=pchcol(l, 1, f))
                ACT(e_.ap[:, 0:T], e_.ap[:, 0:T], AF.Tanh, [e_.t, pch.t], [e_.t], scale=pchcol(l, 0, f), bias=pchcol(l, 1, f))
                st.update(wq=wq)

            def s2_():
                P.phase = "B4"
                e_, wq, s2 = st["e"], st["wq"], st["s2"]
                STT(wq.ap[:, 0:T], e_.ap[:, 0:T], 1.0, wq.ap[:, 0:T], ALU.add, ALU.mult, [e_.t, wq.t], [wq.t])
                TT(brB.ap[:, f, 0:T], wq.ap[:, 0:T], s2.ap[:, 0:T], ALU.mult, [wq.t, s2.t], [brB.s(f)], eng=peng())

            return [s0, s1, s2_]

        def c2_item(hh):
            st = {}

            def s0():
                P.phase = "C2"
                PT = pt_rot.next()
                for mt in range(2):
                    sp = zrot.next()
                    for kk in range(2):
                        MM(sp.ap[:, 0:T], KT[l].ap[:, 2 * hh + kk, mt * 128:(mt + 1) * 128], qm.ap[:, 2 * hh + kk, 0:T], kk == 0, kk == 1,
                           [KT[l].t, qm.s(2 * hh + kk)], [sp.t])
                    ACT(PT.ap[:, mt, 0:T], sp.ap[:, 0:T], AF.Exp, [sp.t], [PT.t], scale=1.0 / 16.0)
                st.update(PT=PT)

            def s1():
                P.phase = "C2"
                PT = st["PT"]
                dp = zrot.next()
                for mt in range(2):
                    MM(dp.ap[:, 0:T], ones_b.ap[:, :], PT.ap[:, mt, 0:T], mt == 0, mt == 1, [ones_b.t, PT.t], [dp.t])
                rden = tmp.next()
                RCP(rden.ap[:, 0:T], dp.ap[:, 0:T], [dp.t], [rden.t])
                ops_, s2s = [], []
                for kk in range(2):
                    f = 2 * hh + kk
                    op_ = zrot.next()
                    for mt in range(2):
                        MM(op_.ap[:, 0:T], VV[l].ap[:, mt, f * 128:(f + 1) * 128], PT.ap[:, mt, 0:T], mt == 0, mt == 1, [VV[l].t, PT.t], [op_.t])
                    o_ = tmp.next()
                    TT(o_.ap[:, 0:T], op_.ap[:, 0:T], rden.ap[:, 0:T], ALU.mult, [op_.t, rden.t], [o_.t])
                    ops_.append(o_)
                for kk in range(2):
                    f = 2 * hh + kk
                    wb, wv = wblk(l, BLK_CG + f // 4)
                    zcg = ztile_fm(wv, wb, (f % 4) * 128, hT.ap, T, hTt)
                    s2 = s2rot.next()
                    ACT(s2.ap[:, 0:T], zcg.ap[:, 0:T], AF.Tanh, [zcg.t], [s2.t], scale=0.5)
                    STT(s2.ap[:, 0:T], s2.ap[:, 0:T], 1.0, zcg.ap[:, 0:T], ALU.add, ALU.mult, [s2.t, zcg.t], [s2.t])
                    s2s.append(s2)
                st.update(o=ops_, s2=s2s)

            def s2_():
                P.phase = "C2"
                for kk in range(2):
                    f = 2 * hh + kk
                    TT(brC3[:, f, 0:T], st["o"][kk].ap[:, 0:T], st["s2"][kk].ap[:, 0:T], ALU.mult,
                       [st["o"][kk].t, st["s2"][kk].t], [brC.s(f)], eng=peng())

            return [s0, s1, s2_]

        items = []
        for i in range(4):
            items += [b4_item(2 * i), b4_item(2 * i + 1), c2_item(i)]
        nst = 3
        for step in range(len(items) + nst - 1):
            for sg in reversed(range(nst)):
                it = step - sg
                if 0 <= it < len(items):
                    items[it][sg]()
        P.phase = "M"
        brs = [(brA.ap, [brA.s(f) for f in range(8)]), (brB.ap, [brB.s(f) for f in range(8)]), (brC3, [brC.s(f) for f in range(8)])]
        for half in range(2):
            for n in range(3):
                wbb, wbv = wblk(l, BLK_BR + n * 2 + half)
                wgb, wgv = wblk(l, BLK_MG + n * 2 + half)
                for f4 in range(4):
                    f = half * 4 + f4
                    pp = ztile_fm(wbv, wbb, f4 * 128, brs[n][0], T, brs[n][1])
                    gp = ztile_fm(wgv, wgb, f4 * 128, hT.ap, T, hTt)
                    th = tmp.next()
                    ACT(th.ap[:, 0:T], gp.ap[:, 0:T], AF.Tanh, [gp.t], [th.t], scale=0.5)
                    if n == 0:
                        STT(macc[f4].ap[:, 0:T], th.ap[:, 0:T], 1.0, pp.ap[:, 0:T], ALU.add, ALU.mult, [th.t, pp.t], [macc[f4].t])
                    else:
                        STT(th.ap[:, 0:T], th.ap[:, 0:T], 1.0, pp.ap[:, 0:T], ALU.add, ALU.mult, [th.t, pp.t], [th.t])
                        if n == 1:
                            TT(macc[f4].ap[:, 0:T], macc[f4].ap[:, 0:T], th.ap[:, 0:T], ALU.add, [macc[f4].t, th.t], [macc[f4].t], eng=peng())
                        else:
                            TT(qm.ap[:, f, 0:T], macc[f4].ap[:, 0:T], th.ap[:, 0:T], ALU.add, [macc[f4].t, th.t], [qm.s(f)], eng=peng())

    def fin_chunk(c, tile_idx):
        P.phase = "FIN"
        k = 4 * c
        ACT(yB4[:, c, :], x_tm.ap[:, c, :], AF.Square, [x_tm.s(c)], [yB.s(2 * c), yB.s(2 * c + 1), smF.s(c)], accum=smF.ap[:, k:k + 1])
        if FSQ:
            ACT(smF.ap[:, k + 2:k + 3], smF.ap[:, k:k + 1], AF.Sqrt, [smF.s(c), epsR.t], [smF.s(c)], scale=1.0 / D, bias=epsR.ap[:, 0:1])
        else:
            TS(smF.ap[:, k + 1:k + 2], smF.ap[:, k:k + 1], 1.0 / D, RMS_EPS, ALU.mult, ALU.add, [smF.s(c)], [smF.s(c)])
            ACT(smF.ap[:, k + 2:k + 3], smF.ap[:, k + 1:k + 2], AF.Sqrt, [smF.s(c)], [smF.s(c)])
        RCP(smF.ap[:, k + 3:k + 4], smF.ap[:, k + 2:k + 3], [smF.s(c)], [smF.s(c)])
        STT(yB4[:, c, :], x_tm.ap[:, c, :], smF.ap[:, k + 3:k + 4], gbn[2].ap[:, :], ALU.mult, ALU.mult,
            [x_tm.s(c), smF.s(c), gbn[2].t], [yB.s(2 * c), yB.s(2 * c + 1)])
        r0 = tile_idx * 512 + c * 128
        P.dma("sp", None, lambda e: e.dma_start(out=y_out[r0:r0 + 128, :], in_=yB4[:, c, :]), key=f"st{c}",
              reads=[yB.s(2 * c), yB.s(2 * c + 1)], writes=[P.tok(f"yout{c}")], final=True, nbytes=1 << 19)
        if tile_idx + 1 < 8:
            load_x_chunk(2 + 4 * (tile_idx + 1) + c, c)

    def o_phase(l, nch, nxt, tile_idx=None):
        qtoks = [qm.s(f) for f in range(8)]
        pend = []
        for c in range(nch):
            P.phase = "O"
            for h in range(2):
                wb, wv = wblk(l, BLK_OUT + h)
                zp = zrot.next()
                for kt in range(8):
                    MM(zp.ap[:, :], qm.ap[:, kt, c * 128:(c + 1) * 128], wv[:, kt, :], kt == 0, kt == 7, qtoks + [wb.t], [zp.t])
                STT(x_tm.ap[:, c, h * 512:(h + 1) * 512], zp.ap[:, :], 0.25, x_tm.ap[:, c, h * 512:(h + 1) * 512], ALU.mult, ALU.add,
                    [zp.t, x_tm.s(c)], [x_tm.s(c)])
            if nxt[0] == "p0":
                P.phase = "P0"
                xs = xs_rot.next()
                p0_stage1(c, nxt[1], xs)
                pend.append((c, xs))
                if len(pend) > 2:
                    p0_stage2(*pend.pop(0))
            elif nxt[0] == "fin":
                fin_chunk(c, tile_idx)
        P.phase = "P0"
        while pend:
            p0_stage2(*pend.pop(0))

    for l in range(2):
        rms_to_hT(2, (yB4[:, 2 + l, :], [yB.s(4 + 2 * l), yB.s(5 + 2 * l)]))
        P.phase = "KV"
        for f in range(8):
            wb, wv = wblk(l, BLK_KV + f // 4)
            zp = ztile_fm(wv, wb, (f % 4) * 128, hT.ap, MEM, hTt)
            CP(KT[l].ap[:, f, :], zp.ap[:, 0:MEM], [zp.t], [KT[l].t])
        for mt in range(2):
            for hb in range(2):
                wb, wv = wblk(l, BLK_KV + 2 + hb)
                zp = zrot.next()
                for kt in range(8):
                    MM(zp.ap[:, :], hT.ap[:, kt, mt * 128:(mt + 1) * 128], wv[:, kt, :], kt == 0, kt == 7, hTt + [wb.t], [zp.t])
                CP(VV[l].ap[:, mt, hb * 512:(hb + 1) * 512], zp.ap[:, :], [zp.t], [VV[l].t])
    for l in range(2):
        toks = [glu[l].s(f) for f in range(8)]
        P.op("dve", lambda e, l=l: e.memset(glu[l].ap[:, :, 0:30], 0.0), [], toks)

    def load_x_chunk(gc, c):
        P.dma("sp", None, lambda e: e.dma_start(out=x_tm.ap[:, c, :], in_=x_in[gc * 128:(gc + 1) * 128, :]), key=f"xl{c}",
              writes=[x_tm.s(c)], nbytes=1 << 19)

    g0 = (gbn[0].ap[:, :], [gbn[0].t])
    g1 = (gbn[1].ap[:, :], [gbn[1].t])
    load_x_chunk(0, 0)
    rms_to_hT(1, g0)
    phase_B1(0, 1)
    halo_copy(0, 1)
    load_x_chunk(1, 0)
    rms_to_hT(1, g0)
    layer_body(0, 1)
    o_phase(0, 1, ("p0", g1))
    phase_B1(1, 1)
    halo_copy(1, 1)
    for c in range(4):
        load_x_chunk(2 + c, c)
    for t in range(8):
        OPT["pool"] = t >= 1
        rms_to_hT(4, g0)
        if t == 0 and PACE:
            issue_casts(order2, extra_reads=hTt)
        layer_body(0, 4)
        o_phase(0, 4, ("p0", g1))
        layer_body(1, 4)
        o_phase(1, 4, ("fin",), tile_idx=t)
    if os.environ.get("MK_DEBUG_TAGS"):
        P.names = {}
    P.reorder = os.environ.get("MK_REORDER", "1") == "1"
    P.emit()
    if P.names is not None:
        with open(os.environ["MK_DEBUG_TAGS"], "w") as fh:
            json.dump(P.names, fh)
    return nc, P


def _prep_inputs(x, mem, norm_g, mem_norm_g, w_in, gmlp_ln_g, gmlp_ln_b, w_s, b_s, conv_w, conv_b,
                 conv_ln_g, conv_ln_b, w_kv, w_branch, w_out, final_norm_g):
    f32 = np.float32
    A = lambda a: np.ascontiguousarray(np.asarray(a, dtype=f32))
    x, mem = A(x), A(mem)
    bc = lambda v: np.broadcast_to(np.asarray(v, f32)[None, :], (128, D))
    gb = np.stack([bc(norm_g[0]), bc(norm_g[1]), bc(mem_norm_g[0]), bc(mem_norm_g[1]), bc(final_norm_g),
                   bc(gmlp_ln_b[0]), bc(gmlp_ln_b[1])]).astype(f32)
    fm = lambda v: np.asarray(v, f32).reshape(8, 128).T
    pc = np.zeros((128, 2, 4, 8), f32)
    for l in range(2):
        pc[:, l, 0] = fm(gmlp_ln_g[l])
        pc[:, l, 1] = fm(conv_b[l])
        pc[:, l, 2] = fm(conv_ln_g[l])
        pc[:, l, 3] = fm(conv_ln_b[l])
    cw = np.zeros((128, 2, 8, KW), f32)
    cwa = np.asarray(conv_w, f32)
    for l in range(2):
        cw[:, l] = cwa[l].reshape(KW, 8, 128).transpose(2, 1, 0)
    wst = np.asarray(w_s, f32).transpose(0, 3, 1, 2).reshape(2, 128, 1024)
    bs = np.asarray(b_s, f32).reshape(2, 1, 1024)
    cst = np.zeros((128, 384), f32)
    cst[:, 0:128] = np.eye(128, dtype=f32)
    cst[:, 128:256] = np.triu(np.ones((128, 128), f32))
    cst[:, 256:384] = 1.0
    shared = {
        "w_in": A(w_in), "w_kv": A(w_kv), "w_br": A(w_branch), "w_out": A(w_out),
        "gb_in": np.ascontiguousarray(gb), "pc_in": np.ascontiguousarray(pc.reshape(128, 64)),
        "cw_in": np.ascontiguousarray(cw.reshape(128, 2 * 8 * KW)), "wst_in": np.ascontiguousarray(wst),
        "bs_in": np.ascontiguousarray(bs), "cst_in": cst,
    }
    in_maps = []
    for i in range(NCORES):
        b, half = i // 2, i % 2
        xc = np.zeros((NCH_CORE * 128, D), f32)
        if half == 1:
            xc[:HALO] = x[b, TOK_CORE - HALO:TOK_CORE]
        xc[HALO:] = x[b, half * TOK_CORE:(half + 1) * TOK_CORE]
        m = dict(shared)
        m["x_in"] = xc
        m["mem_in"] = mem[b]
        in_maps.append(m)
    return in_maps


_CACHE = {}


def kernel(**inputs):
    in_maps = _prep_inputs(**inputs)
    if "nc" not in _CACHE:
        _CACHE["nc"] = build_program()[0]
    nc = _CACHE["nc"]
    res = run_bass_kernel_spmd(nc, in_maps, core_ids=list(range(NCORES)))
    out = np.empty((BATCH, SEQ, D), np.float32)
    for i in range(NCORES):
        b, half = i // 2, i % 2
        out[b, half * TOK_CORE:(half + 1) * TOK_CORE] = np.asarray(res.results[i]["y_out"], np.float32)
    return out
```
